# Optimizing a Trainium2 kernel written in Bass

```python
import math
import jax, jax.numpy as jnp
from jax import lax
import numpy as np

D_MODEL = 1024
BATCH = 2
SEQ = 8192
DEPTH = 2

GM_WIDTH = D_MODEL // 2
GM_GROUPS = 8
GM_CHUNK = 128
NSA_HEADS = 8
NSA_KV_GROUPS = 2
NSA_HPG = NSA_HEADS // NSA_KV_GROUPS
HEAD_DIM = 64
NSA_WIDTH = NSA_HEADS * HEAD_DIM
KV_WIDTH = NSA_KV_GROUPS * HEAD_DIM
CMP_BLOCK = 32
CMP_STRIDE = 16
CMP_HIDDEN = 256
SEL_BLOCK = 64
N_SEL = 16
WINDOW = 512
Q_BLOCK = 128
N_NSA_BRANCHES = 3
N_MERGE_BRANCHES = 2
FORCED_SCORE = 1e6
NUM_BUCKETS = 32
MAX_DISTANCE = 128
D_FF = 2816
CONV_WIDTH = 3
ALPHA = (2.0 * DEPTH) ** 0.25
BETA = (8.0 * DEPTH) ** -0.25
LN_EPS = 1e-5
NEG_INF = -1e30

IN_SEGMENTS = (GM_WIDTH, GM_WIDTH, NSA_WIDTH, KV_WIDTH, KV_WIDTH, KV_WIDTH, KV_WIDTH, KV_WIDTH, KV_WIDTH,
               NSA_HEADS * N_NSA_BRANCHES, D_MODEL * N_MERGE_BRANCHES)
IN_SCALES = (BETA, BETA, 1.0, 1.0, BETA, 1.0, BETA, 1.0, BETA, 1.0, 1.0)
D_IN = sum(IN_SEGMENTS)

kernel_name = 'hybrid_gmlp_nsa_convffn_deepnorm'


def layer_norm(x, g, b):
    xf = x.astype(jnp.float32)
    mu = xf.mean(-1, keepdims=True)
    var = jnp.square(xf - mu).mean(-1, keepdims=True)
    return ((xf - mu) * lax.rsqrt(var + LN_EPS) * g + b).astype(x.dtype)


def split_columns(z, sizes):
    offsets = np.cumsum(sizes)[:-1].tolist()
    return jnp.split(z, offsets, axis=-1)


def t5_bucket(dist):
    n = jnp.maximum(dist, 0)
    max_exact = NUM_BUCKETS // 2
    log_ratio = jnp.log(jnp.maximum(n, 1).astype(jnp.float32) / max_exact) / math.log(MAX_DISTANCE / max_exact)
    large = jnp.minimum(max_exact + (log_ratio * (NUM_BUCKETS - max_exact)).astype(jnp.int32), NUM_BUCKETS - 1)
    return jnp.where(n < max_exact, n, large)


def shared_bias(rel_bias, dist):
    b = jnp.moveaxis(rel_bias[t5_bucket(dist)], -1, 0)
    return b.reshape(NSA_KV_GROUPS, NSA_HPG, *dist.shape)


def masked_softmax(logits, valid):
    logits = jnp.where(valid, logits.astype(jnp.float32), NEG_INF)
    return jax.nn.softmax(logits, axis=-1) * valid


def selection_overlap(seq):
    n_cmp = seq // CMP_STRIDE - 1
    n_blk = seq // SEL_BLOCK
    cs = np.arange(n_cmp)[:, None] * CMP_STRIDE
    ss = np.arange(n_blk)[None, :] * SEL_BLOCK
    ov = np.minimum(cs + CMP_BLOCK, ss + SEL_BLOCK) - np.maximum(cs, ss)
    return jnp.asarray(np.maximum(ov, 0) / CMP_BLOCK, dtype=jnp.float32)


def gmlp_spatial_gating(u, v, ln_g, ln_b, w_s, b_s):
    B, S, _ = u.shape
    gd = GM_WIDTH // GM_GROUPS
    v = layer_norm(v, ln_g, ln_b).reshape(B, S // GM_CHUNK, GM_CHUNK, GM_GROUPS, gd)
    causal = jnp.tril(jnp.ones((GM_CHUNK, GM_CHUNK), w_s.dtype))
    sv = jnp.einsum('gts,bnsgc->bntgc', w_s * causal, v) + b_s.T[None, None, :, :, None]
    return u * sv.reshape(B, S, GM_WIDTH)


def compress(kv, pos, w1, b1, w2):
    B, G, S, dh = kv.shape
    chunks = kv.reshape(B, G, S // CMP_STRIDE, CMP_STRIDE, dh)
    blocks = jnp.concatenate([chunks[:, :, :-1], chunks[:, :, 1:]], axis=3)
    blocks = (blocks + pos).reshape(B, G, -1, CMP_BLOCK * dh)
    return jax.nn.silu(blocks @ w1 + b1) @ w2


def native_sparse_attention(q, kc, vc, ks, vs, kw, vw, gate, rel_bias):
    B, G, HPG, S, dh = q.shape
    n_cmp = kc.shape[2]
    n_blk = S // SEL_BLOCK
    n_sel = min(N_SEL, n_blk)
    scale = dh ** -0.5
    overlap = selection_overlap(S)
    cmp_end = jnp.arange(n_cmp) * CMP_STRIDE + CMP_BLOCK - 1
    ks_blocks = ks.reshape(B, G, n_blk, SEL_BLOCK, dh)
    vs_blocks = vs.reshape(B, G, n_blk, SEL_BLOCK, dh)
    kw_pad = jnp.pad(kw, ((0, 0), (0, 0), (WINDOW, 0), (0, 0)))
    vw_pad = jnp.pad(vw, ((0, 0), (0, 0), (WINDOW, 0), (0, 0)))
    bias_gh = rel_bias.reshape(NUM_BUCKETS, G, HPG)
    b_ix = jnp.arange(B)[:, None, None, None]
    g_ix = jnp.arange(G)[None, :, None, None]
    blk = jnp.arange(n_blk)
    group_bias = jax.vmap(lambda d, tab: tab[t5_bucket(d)], in_axes=(1, 1), out_axes=1)

    def step(i):
        qs = i * Q_BLOCK
        t = qs + jnp.arange(Q_BLOCK)
        qb = lax.dynamic_slice_in_dim(q, qs, Q_BLOCK, axis=3) * scale
        gb = jax.nn.sigmoid(lax.dynamic_slice_in_dim(gate, qs, Q_BLOCK, axis=3))

        dist_c = t[:, None] - cmp_end[None, :]
        logits = jnp.einsum('bghqd,bgkd->bghqk', qb, kc) + shared_bias(rel_bias, dist_c)
        p_c = masked_softmax(logits, dist_c >= 0)
        o_c = jnp.einsum('bghqk,bgkd->bghqd', p_c.astype(vc.dtype), vc)

        imp = jnp.einsum('bghqk,kj->bgqj', p_c, overlap)
        cur = t[:, None] // SEL_BLOCK
        forced = (blk == 0) | (blk == cur) | (blk == cur - 1)
        imp = jnp.where(blk > cur, NEG_INF, jnp.where(forced, FORCED_SCORE, imp))
        _, idx = lax.top_k(imp, n_sel)

        k_sel = ks_blocks[b_ix, g_ix, idx].reshape(B, G, Q_BLOCK, n_sel * SEL_BLOCK, dh)
        v_sel = vs_blocks[b_ix, g_ix, idx].reshape(B, G, Q_BLOCK, n_sel * SEL_BLOCK, dh)
        tok = (idx[..., None] * SEL_BLOCK + jnp.arange(SEL_BLOCK)).reshape(B, G, Q_BLOCK, -1)
        dist_s = t[:, None] - tok
        bias_s = jnp.moveaxis(group_bias(dist_s, bias_gh), -1, 2)
        logits = jnp.einsum('bghqd,bgqkd->bghqk', qb, k_sel) + bias_s
        p_s = masked_softmax(logits, (dist_s >= 0)[:, :, None])
        o_s = jnp.einsum('bghqk,bgqkd->bghqd', p_s.astype(v_sel.dtype), v_sel)

        kwb = lax.dynamic_slice_in_dim(kw_pad, qs, WINDOW + Q_BLOCK, axis=2)
        vwb = lax.dynamic_slice_in_dim(vw_pad, qs, WINDOW + Q_BLOCK, axis=2)
        kpos = qs - WINDOW + jnp.arange(WINDOW + Q_BLOCK)
        dist_w = t[:, None] - kpos[None, :]
        valid_w = (dist_w >= 0) & (dist_w < WINDOW) & (kpos >= 0)[None, :]
        logits = jnp.einsum('bghqd,bgkd->bghqk', qb, kwb) + shared_bias(rel_bias, dist_w)
        p_w = masked_softmax(logits, valid_w)
        o_w = jnp.einsum('bghqk,bgkd->bghqd', p_w.astype(vwb.dtype), vwb)

        return gb[..., 0:1] * o_c + gb[..., 1:2] * o_s + gb[..., 2:3] * o_w

    out = lax.map(step, jnp.arange(S // Q_BLOCK))
    return out.transpose(1, 0, 4, 2, 3, 5).reshape(B, S, G * HPG * dh)


def token_mixers(h, w_in, b_in, gm_ln_g, gm_ln_b, gm_ws, gm_bs, cmp_pos, cmp_w1, cmp_b1, cmp_w2,
                 rel_bias, w_proj_a, w_proj_b, w_out):
    B, S, _ = h.shape
    proj = h @ w_in + b_in
    u, v, q, kc, vc, ks, vs, kw, vw, ng, mg = split_columns(proj, IN_SEGMENTS)

    a = gmlp_spatial_gating(jax.nn.gelu(u), jax.nn.gelu(v), gm_ln_g, gm_ln_b, gm_ws, gm_bs)

    def kv_heads(z):
        return z.reshape(B, S, NSA_KV_GROUPS, HEAD_DIM).transpose(0, 2, 1, 3)
    qh = q.reshape(B, S, NSA_KV_GROUPS, NSA_HPG, HEAD_DIM).transpose(0, 2, 3, 1, 4)
    kc = compress(kv_heads(kc), cmp_pos[0], cmp_w1[0], cmp_b1[0], cmp_w2[0])
    vc = compress(kv_heads(vc), cmp_pos[1], cmp_w1[1], cmp_b1[1], cmp_w2[1])
    gate = ng.reshape(B, S, NSA_KV_GROUPS, NSA_HPG, N_NSA_BRANCHES).transpose(0, 2, 3, 1, 4)
    o = native_sparse_attention(qh, kc, vc, kv_heads(ks), kv_heads(vs), kv_heads(kw), kv_heads(vw), gate, rel_bias)

    g_a, g_b = jnp.split(mg, N_MERGE_BRANCHES, axis=-1)
    y = jax.nn.sigmoid(g_a) * (a @ w_proj_a) + jax.nn.sigmoid(g_b) * (o @ w_proj_b)
    return y @ w_out


def conv_ffn(h, w_up, conv_w, conv_b, w_down):
    S = h.shape[1]
    up = h @ w_up
    pad = jnp.pad(up, ((0, 0), (CONV_WIDTH - 1, 0), (0, 0)))
    c = conv_b + conv_w[0] * pad[:, 0:S] + conv_w[1] * pad[:, 1:S + 1] + conv_w[2] * pad[:, 2:S + 2]
    g, val = jnp.split(c, 2, axis=-1)
    return (jax.nn.silu(g) * val) @ w_down


def setup_inputs(seed: int = 0) -> dict:
    key = jax.random.key(seed)
    ks = jax.random.split(key, 24)
    L = DEPTH

    def nrm(k, shape, scale):
        return jax.random.normal(k, shape, jnp.float32) * scale

    seg_keys = jax.random.split(ks[1], len(IN_SEGMENTS))
    w_in = jnp.concatenate([nrm(sk, (L, D_MODEL, n), s * D_MODEL ** -0.5)
                            for sk, n, s in zip(seg_keys, IN_SEGMENTS, IN_SCALES)], axis=-1)
    return {
        'x': nrm(ks[0], (BATCH, SEQ, D_MODEL), 1.0),
        'w_in': w_in,
        'b_in': nrm(ks[2], (L, D_IN), 0.02),
        'gm_ln_g': 1.0 + nrm(ks[3], (L, GM_WIDTH), 0.05),
        'gm_ln_b': nrm(ks[4], (L, GM_WIDTH), 0.02),
        'gm_ws': nrm(ks[5], (L, GM_GROUPS, GM_CHUNK, GM_CHUNK), GM_CHUNK ** -0.5),
        'gm_bs': 1.0 + nrm(ks[6], (L, GM_GROUPS, GM_CHUNK), 0.05),
        'cmp_pos': nrm(ks[7], (L, 2, CMP_BLOCK, HEAD_DIM), 0.5),
        'cmp_w1': nrm(ks[8], (L, 2, CMP_BLOCK * HEAD_DIM, CMP_HIDDEN), (CMP_BLOCK * HEAD_DIM) ** -0.5),
        'cmp_b1': nrm(ks[9], (L, 2, CMP_HIDDEN), 0.02),
        'cmp_w2': nrm(ks[10], (L, 2, CMP_HIDDEN, HEAD_DIM), CMP_HIDDEN ** -0.5),
        'rel_bias': nrm(ks[11], (NUM_BUCKETS, NSA_HEADS), 0.5),
        'w_proj_a': nrm(ks[12], (L, GM_WIDTH, D_MODEL), BETA * GM_WIDTH ** -0.5),
        'w_proj_b': nrm(ks[13], (L, NSA_WIDTH, D_MODEL), BETA * NSA_WIDTH ** -0.5),
        'w_out': nrm(ks[14], (L, D_MODEL, D_MODEL), BETA * D_MODEL ** -0.5),
        'ln1_g': 1.0 + nrm(ks[15], (L, D_MODEL), 0.05),
        'ln1_b': nrm(ks[16], (L, D_MODEL), 0.02),
        'ffn_w_up': nrm(ks[17], (L, D_MODEL, 2 * D_FF), BETA * D_MODEL ** -0.5),
        'ffn_conv_w': nrm(ks[18], (L, CONV_WIDTH, 2 * D_FF), CONV_WIDTH ** -0.5),
        'ffn_conv_b': nrm(ks[19], (L, 2 * D_FF), 0.02),
        'ffn_w_down': nrm(ks[20], (L, D_FF, D_MODEL), BETA * D_FF ** -0.5),
        'ln2_g': 1.0 + nrm(ks[21], (L, D_MODEL), 0.05),
        'ln2_b': nrm(ks[22], (L, D_MODEL), 0.02),
    }


def reference(x, w_in, b_in, gm_ln_g, gm_ln_b, gm_ws, gm_bs, cmp_pos, cmp_w1, cmp_b1, cmp_w2, rel_bias,
              w_proj_a, w_proj_b, w_out, ln1_g, ln1_b, ffn_w_up, ffn_conv_w, ffn_conv_b, ffn_w_down,
              ln2_g, ln2_b):
    h = x
    for l in range(DEPTH):
        m = token_mixers(h, w_in[l], b_in[l], gm_ln_g[l], gm_ln_b[l], gm_ws[l], gm_bs[l], cmp_pos[l],
                         cmp_w1[l], cmp_b1[l], cmp_w2[l], rel_bias, w_proj_a[l], w_proj_b[l], w_out[l])
        h = layer_norm(ALPHA * h + m, ln1_g[l], ln1_b[l])
        f = conv_ffn(h, ffn_w_up[l], ffn_conv_w[l], ffn_conv_b[l], ffn_w_down[l])
        h = layer_norm(ALPHA * h + f, ln2_g[l], ln2_b[l])
    return h
```

```python
import math
import contextlib
import numpy as np
import concourse.bass as bass
import concourse.mybir as mybir
from concourse.bass_utils import run_bass_kernel_spmd

F32 = mybir.dt.float32
BF16 = mybir.dt.bfloat16
AF = mybir.ActivationFunctionType
ALU = mybir.AluOpType

D = 1024
SEQ = 8192
NB = 16
DFF = 2816
ALPHA = (2.0 * 2) ** 0.25
EPS = 1e-5
NEG = -30000.0
GC = 1.5957691216057308


class Buf:
    def __init__(self, name):
        self.name = name
        self.w = None
        self.r = {}
        self.dsem = None
        self.dcnt = 0


class Sched:
    ENG = ["sync", "tensor", "vector", "scalar", "gpsimd"]

    def __init__(self, nc, stack):
        self.nc = nc
        self.stack = stack
        self.prog = {e: [] for e in self.ENG}
        self.sem = {e: stack.enter_context(nc.semaphore("sem_" + e)) for e in self.ENG}
        self.cnt = {e: 0 for e in self.ENG}
        self.waited = {e: {} for e in self.ENG}
        self.dma_bufs = []
        self.nsem = 0

    def _waits(self, eng, reads, writes):
        deps = []
        for b in reads:
            if b.w is not None:
                deps.append(b.w)
        for b in writes:
            if b.w is not None:
                deps.append(b.w)
            deps.extend(b.r.values())
        out = {}
        for (sem, val, src) in deps:
            if src == eng and eng == "tensor":
                continue
            k = id(sem)
            if self.waited[eng].get(k, 0) >= val:
                continue
            if k not in out or out[k][1] < val:
                out[k] = (sem, val)
        for k, (sem, val) in out.items():
            self.waited[eng][k] = val
        return list(out.values())

    def op(self, eng, fn, reads=(), writes=()):
        waits = self._waits(eng, reads, writes)
        self.cnt[eng] += 1
        pt = (self.sem[eng], self.cnt[eng], eng)
        self.prog[eng].append((waits, fn, self.sem[eng], 1))
        for b in reads:
            b.r[id(pt[0])] = pt
        for b in writes:
            b.w = pt
            b.r = {}

    def dma(self, q, fn, reads=(), writes=()):
        waits = self._waits(q, reads, writes)
        tgt = writes[0]
        if tgt.dsem is None:
            tgt.dsem = self.stack.enter_context(self.nc.semaphore("dsem_%d" % self.nsem))
            self.nsem += 1
            self.dma_bufs.append(tgt)
        tgt.dcnt += 16
        pt = (tgt.dsem, tgt.dcnt, "dma")
        self.prog[q].append((waits, fn, tgt.dsem, 16))
        for b in reads:
            b.r[id(pt[0])] = pt
        for b in writes:
            b.w = pt
            b.r = {}

    def wait_all_dma(self, eng="sync"):
        ws = []
        for b in self.dma_bufs:
            if b.dcnt > 0 and self.waited[eng].get(id(b.dsem), 0) < b.dcnt:
                ws.append((b.dsem, b.dcnt))
                self.waited[eng][id(b.dsem)] = b.dcnt
        if ws:
            self.prog[eng].append((ws, None, None, 0))

    def fence(self):
        pts = [(self.sem[e], self.cnt[e]) for e in self.ENG if self.cnt[e] > 0]
        for e in self.ENG:
            ws = []
            for (sem, val) in pts:
                if sem is self.sem[e]:
                    continue
                if self.waited[e].get(id(sem), 0) < val:
                    ws.append((sem, val))
                    self.waited[e][id(sem)] = val
            for b in self.dma_bufs:
                if b.dcnt > 0 and self.waited[e].get(id(b.dsem), 0) < b.dcnt:
                    ws.append((b.dsem, b.dcnt))
                    self.waited[e][id(b.dsem)] = b.dcnt
            if ws:
                self.prog[e].append((ws, None, None, 0))

    def flush(self):
        self.fence()
        prog = self.prog
        self.prog = {e: [] for e in self.ENG}
        with self.nc.Block() as block:
            def mk(ename):
                items = prog[ename]

                def body(e):
                    for (waits, fn, sem, amt) in items:
                        for (s, v) in waits:
                            e.wait_ge(s, v)
                        if fn is not None:
                            ins = fn(e)
                            ins.then_inc(sem, amt)
                return body
            block.sync(mk("sync"))
            block.tensor(mk("tensor"))
            block.vector(mk("vector"))
            block.scalar(mk("scalar"))
            block.gpsimd(mk("gpsimd"))


class Ctx:
    pass


def new_ctx():
    nc = bass.Bass("TRN2", target_bir_lowering=False)
    c = Ctx()
    c.nc = nc
    c.top = contextlib.ExitStack()
    c.S = Sched(nc, c.top)
    c.ins = {}
    c.ps = [c.top.enter_context(nc.psum_tensor("ps%d" % i, [128, 512], F32)) for i in range(8)]
    c.psb = [Buf("ps%d" % i) for i in range(8)]
    return c


def dram_in(c, name, shape):
    ap = c.nc.dram_tensor(name, list(shape), F32, kind="ExternalInput").ap()
    c.ins[name] = ap
    return ap


def sbt(c, st, name, shape, d=BF16):
    t = st.enter_context(c.nc.sbuf_tensor("sb_" + name, list(shape), d))
    return t, Buf(name)


def load(c, q, dst_ap, dst_buf, src_ap):
    c.S.dma(q, lambda e: e.dma_start(out=dst_ap, in_=src_ap), reads=(), writes=(dst_buf,))


def mm(c, out_ap, out_buf, lhsT, rhs, start, stop, reads):
    c.S.op("tensor", lambda e: e.matmul(out_ap, lhsT=lhsT, rhs=rhs, start=start, stop=stop, skip_group_check=True),
           reads=reads, writes=(out_buf,))


def gelu_chain(c, x_ap, xb, t_ap, tb, out_ap, outb):
    S = c.S
    S.op("vector", lambda e: e.tensor_tensor(out=t_ap, in0=x_ap, in1=x_ap, op=ALU.mult), reads=(xb,), writes=(tb,))
    S.op("vector", lambda e: e.tensor_scalar(out=t_ap, in0=t_ap, scalar1=0.044715, scalar2=1.0, op0=ALU.mult, op1=ALU.add),
         reads=(tb,), writes=(tb,))
    S.op("vector", lambda e: e.tensor_tensor(out=t_ap, in0=t_ap, in1=x_ap, op=ALU.mult), reads=(tb, xb), writes=(tb,))
    S.op("scalar", lambda e: e.activation(out=t_ap, in_=t_ap, func=AF.Sigmoid, scale=GC), reads=(tb,), writes=(tb,))
    S.op("vector", lambda e: e.tensor_tensor(out=out_ap, in0=x_ap, in1=t_ap, op=ALU.mult), reads=(xb, tb), writes=(outb,))


def layer_norm_rows(c, st, r, rb, nhalf, g_t, gb, b_t, bb, out_ap, outb, tag):
    S = c.S
    stats, sb_ = sbt(c, st, "lnst" + tag, [128, nhalf, 6], F32)
    mv, mvb = sbt(c, st, "lnmv" + tag, [128, 2], F32)
    rs, rsb = sbt(c, st, "lnrs" + tag, [128, 1], F32)

    def emit():
        for hf in range(nhalf):
            S.op("vector", lambda e, hf=hf: e.bn_stats(out=stats[:, hf, :], in_=r[:, hf * 512:(hf + 1) * 512]),
                 reads=(rb,), writes=(sb_,))
        S.op("vector", lambda e: e.bn_aggr(out=mv[:], in_=stats[:].rearrange("p a b -> p (a b)")), reads=(sb_,), writes=(mvb,))
        S.op("scalar", lambda e: e.activation(out=rs[:], in_=mv[:, 1:2], func=AF.Sqrt, bias=EPS, scale=1.0), reads=(mvb,), writes=(rsb,))
        S.op("vector", lambda e: e.reciprocal(out=rs[:], in_=rs[:]), reads=(rsb,), writes=(rsb,))
        S.op("vector", lambda e: e.tensor_scalar(out=r[:], in0=r[:], scalar1=mv[:, 0:1], scalar2=rs[:, 0:1],
                                                 op0=ALU.subtract, op1=ALU.mult), reads=(rb, mvb, rsb), writes=(rb,))
        S.op("vector", lambda e: e.tensor_tensor(out=r[:], in0=r[:], in1=g_t[:], op=ALU.mult), reads=(rb, gb), writes=(rb,))
        S.op("vector", lambda e: e.tensor_tensor(out=out_ap, in0=r[:], in1=b_t[:], op=ALU.add), reads=(rb, bb), writes=(outb,))
    return emit


DBG = {}


def build_B():
    c = new_ctx()
    nc, S, ps, psb = c.nc, c.S, c.ps, c.psb
    top = c.top
    I = lambda n, s: dram_in(c, n, s)
    hT_full = I("hT_full", [D, SEQ]); hT_own = I("hT_own", [D, 2048]); h_own = I("h_own", [2048, D])
    w_kvf = I("w_kvf", [D, 512]); w_kvt = I("w_kvt", [D, 256]); b_kvf = I("b_kvf", [1, 512]); b_kvt = I("b_kvt", [1, 256]); b_kvfc = I("b_kvfc", [128, 4])
    w_u = I("w_u", [D, 512]); w_v = I("w_v", [D, 512]); w_q = I("w_q", [D, 512]); w_ng = I("w_ng", [D, 24])
    b_u = I("b_u", [1, 512]); b_v = I("b_v", [1, 512]); b_q = I("b_q", [1, 512]); b_ng = I("b_ng", [1, 24]); b_qc = I("b_qc", [128, 4])
    w_mg = I("w_mg", [D, 2048]); b_mg = I("b_mg", [1, 2048]); b_mgc = I("b_mgc", [128, 16])
    lng = I("lng", [128, 512]); lnb = I("lnb", [128, 512])
    wsT = I("wsT", [128, 8, 128]); triu = I("triu", [128, 128]); bsb_d = I("bsb", [128, 4, 128])
    w1d = I("w1d", [128, 2, 32, 256]); posT_d = I("posT", [64, 2, 34]); b1_d = I("b1", [128, 2, 2])
    w2k_d = I("w2k", [128, 2, 128]); w2v_d = I("w2v", [128, 2, 64]); ovl_d = I("ovl", [128, 4, 128])
    w_pa = I("w_pa", [512, D]); w_pb = I("w_pb", [512, D]); w_o = I("w_o", [D, D])
    ln1g = I("ln1g", [128, D]); ln1b = I("ln1b", [128, D])
    bsel_d = I("bsel", [2, 5, 128, 512]); bwin_d = I("bwin", [2, 8, 128, 512]); bcmp_d = I("bcmp", [2, 40, 512])
    far_d = I("far", [2, 128, 512]); nsel_d = I("nsel", [5, 128, 512]); nwin_d = I("nwin", [8, 128, 512]); ncmp_d = I("ncmp", [40, 512])
    jw_d = I("jw", [40, 272]); i4_d = I("i4", [128, 512]); cand_d = I("cand", [128, 256]); forc_d = I("forc", [128, 256])
    h1_out = nc.dram_tensor("h1_out", [2048, D], F32, kind="ExternalOutput").ap()
    outb = Buf("h1_out")

    ones, onesb = sbt(c, top, "ones", [1, 512], F32)
    identf, identb = sbt(c, top, "identf", [128, 128], F32)
    zer, zerb = sbt(c, top, "zer", [1, 512], BF16)
    kvs = contextlib.ExitStack()
    KsT, KsTb = sbt(c, kvs, "KsT", [128, SEQ]); KwT, KwTb = sbt(c, kvs, "KwT", [128, SEQ])
    Vs, Vsb = sbt(c, kvs, "Vs", [128, 64, 2, 65]); Vw, Vwb = sbt(c, kvs, "Vw", [128, 64, 2, 65])
    KcT, KcTb = sbt(c, kvs, "KcT", [128, 512]); Vca, Vcab = sbt(c, kvs, "Vca", [128, 4, 2, 193])
    WsT, WsTb = sbt(c, kvs, "WsT", [128, 8, 128])
    Msel, Mselb = sbt(c, kvs, "Msel", [128, 2, 5, 512]); Mwin, Mwinb = sbt(c, kvs, "Mwin", [128, 2, 8, 512]); NCm, NCmb = sbt(c, kvs, "NCm", [40, 2, 512])
    aT_d = nc.dram_tensor("aT_scr", [NB, 128, 512], BF16, kind="Internal").ap(); aTdb = Buf("aT_scr")
    oT_d = nc.dram_tensor("oT_scr", [NB, 128, 512], BF16, kind="Internal").ap(); oTdb = Buf("oT_scr")
    S.op("vector", lambda e: e.memset(ones[:], 1.0), writes=(onesb,))
    S.op("vector", lambda e: e.memset(zer[:], 0.0), writes=(zerb,))
    load(c, "sync", identf[:], identb, i4_d[:, 0:128])
    S.op("vector", lambda e: e.memset(Vs[:, :, :, 64:65], 1.0), writes=(Vsb,))
    S.op("vector", lambda e: e.memset(Vw[:, :, :, 64:65], 1.0), writes=(Vwb,))
    S.op("vector", lambda e: e.memset(Vca[:, :, :, 64:65], 1.0), writes=(Vcab,))
    for g in range(2):
        load(c, "gpsimd", Vca[:, :, g, 65:193], Vcab, ovl_d)

    with contextlib.ExitStack() as ph:
        Wkvf, Wkvfb = sbt(c, ph, "Wkvf", [128, 8, 512]); Wkvt, Wkvtb = sbt(c, ph, "Wkvt", [128, 8, 256])
        bkvf, bkvfb = sbt(c, ph, "bkvf", [1, 512], F32); bkvt, bkvtb = sbt(c, ph, "bkvt", [1, 256], F32)
        Kcr, Kcrb = sbt(c, ph, "Kcr", [128, SEQ]); Vcr, Vcrb = sbt(c, ph, "Vcr", [128, SEQ])
        hTf = [sbt(c, ph, "hTf%d" % i, [128, 8, 512]) for i in range(2)]
        W1, W1b = sbt(c, ph, "W1", [128, 2, 32, 256]); posT, posTb = sbt(c, ph, "posTs", [64, 2, 34])
        b1, b1b = sbt(c, ph, "b1s", [128, 2, 2], F32); W2k, W2kb = sbt(c, ph, "W2k", [128, 2, 128]); W2v, W2vb = sbt(c, ph, "W2v", [128, 2, 64])
        hid = [sbt(c, ph, "hid%d" % i, [128, 512]) for i in range(2)]
        cb, cbb = sbt(c, ph, "cb", [128, 2], F32)
        load(c, "gpsimd", Wkvf[:], Wkvfb, w_kvf.rearrange("(kc p) n -> p kc n", p=128))
        load(c, "gpsimd", Wkvt[:], Wkvtb, w_kvt.rearrange("(kc p) n -> p kc n", p=128))
        load(c, "sync", bkvf[:], bkvfb, b_kvf); load(c, "sync", bkvt[:], bkvtb, b_kvt)
        bkvfc, bkvfcb = sbt(c, ph, "bkvfc", [128, 4], F32)
        load(c, "sync", bkvfc[:], bkvfcb, b_kvfc)
        load(c, "gpsimd", W1[:], W1b, w1d); load(c, "gpsimd", posT[:], posTb, posT_d); load(c, "sync", b1[:], b1b, b1_d)
        load(c, "gpsimd", W2k[:], W2kb, w2k_d); load(c, "gpsimd", W2v[:], W2vb, w2v_d)
        stgA, stgAb = sbt(c, ph, "stgA", [128, 512], F32); ngA, ngAb = sbt(c, ph, "ngA", [128, 512], F32); farA, farAb = sbt(c, ph, "farA", [128, 512], F32)
        prep_jobs = []

        def job_tri():
            load(c, "sync", ngA[:, 0:128], ngAb, triu)

        def job_ws(g8):
            def f_():
                load(c, "sync", stgA[:, 0:128], stgAb, wsT[:, g8, :])
                S.op("vector", lambda e: e.tensor_tensor(out=WsT[:, g8, :], in0=stgA[:, 0:128], in1=ngA[:, 0:128], op=ALU.mult), reads=(stgAb, ngAb), writes=(WsTb,))
            return f_

        def job_far(g):
            def f_():
                load(c, "sync", farA[:], farAb, far_d[g])
            return f_

        def job_mask(bsrc, nsrc, dst, dstb, npart):
            def f_():
                load(c, "sync", stgA[0:npart, :], stgAb, bsrc)
                load(c, "sync", ngA[0:npart, :], ngAb, nsrc)
                S.op("vector", lambda e: e.tensor_tensor(out=stgA[0:npart, :], in0=stgA[0:npart, :], in1=farA[0:npart, :], op=ALU.subtract),
                     reads=(stgAb, farAb), writes=(stgAb,))
                S.op("vector", lambda e: e.tensor_tensor(out=dst, in0=stgA[0:npart, :], in1=ngA[0:npart, :], op=ALU.add),
                     reads=(stgAb, ngAb), writes=(dstb,))
            return f_
        prep_jobs.append(job_tri)
        for g8 in range(8):
            prep_jobs.append(job_ws(g8))
        for g in range(2):
            prep_jobs.append(job_far(g))
            for e_ in range(5):
                prep_jobs.append(job_mask(bsel_d[g, e_], nsel_d[e_], Msel[:, g, e_, :], Mselb, 128))
            for e_ in range(8):
                prep_jobs.append(job_mask(bwin_d[g, e_], nwin_d[e_], Mwin[:, g, e_, :], Mwinb, 128))
            prep_jobs.append(job_mask(bcmp_d[g], ncmp_d, NCm[:, g, :], NCmb, 40))
        fdst = [(Kcr, Kcrb), (Vcr, Vcrb), (KsT, KsTb), (KwT, KwTb)]
        for gi in range(16):
            ht, htb = hTf[gi % 2]
            load(c, "gpsimd", ht[:], htb, hT_full[:, gi * 512:(gi + 1) * 512].rearrange("(kc p) t -> p kc t", p=128))
            for ti in range(4):
                b = ti % 2
                for kc in range(8):
                    mm(c, ps[b][:, :], psb[b], Wkvf[:, kc, ti * 128:(ti + 1) * 128], ht[:, kc, :], kc == 0, kc == 7, (Wkvfb, htb))
                dt_, dtb = fdst[ti]
                S.op("scalar", lambda e, b=b, dt_=dt_, gi=gi, ti=ti: e.activation(out=dt_[:, gi * 512:(gi + 1) * 512], in_=ps[b][:, :], func=AF.Identity,
                                                                            bias=bkvfc[:, ti:ti + 1], scale=1.0),
                     reads=(psb[b], bkvfcb), writes=(dtb,))
            for tcn in range(4):
                cidx = 4 * gi + tcn
                b = 2 + tcn % 2
                for kc in range(8):
                    mm(c, ps[b][:, 0:256], psb[b], ht[:, kc, tcn * 128:(tcn + 1) * 128], Wkvt[:, kc, :], kc == 0, False, (Wkvtb, htb))
                mm(c, ps[b][:, 0:256], psb[b], ones[0:1, 0:128], bkvt[0:1, :], False, True, (bkvtb, onesb))
                S.op("vector", lambda e, b=b, cidx=cidx: e.tensor_copy(out=Vs[:, cidx, :, 0:64], in_=ps[b][:, 0:128].rearrange("p (g d) -> p g d", g=2)),
                     reads=(psb[b],), writes=(Vsb,))
                S.op("vector", lambda e, b=b, cidx=cidx: e.tensor_copy(out=Vw[:, cidx, :, 0:64], in_=ps[b][:, 128:256].rearrange("p (g d) -> p g d", g=2)),
                     reads=(psb[b],), writes=(Vwb,))
            for _ in range(3):
                if prep_jobs:
                    prep_jobs.pop(0)()
        while prep_jobs:
            prep_jobs.pop(0)()
        for kv in range(2):
            raw, rawb = (Kcr, Kcrb) if kv == 0 else (Vcr, Vcrb)
            for ft in range(2):
                for r in range(32):
                    mm(c, ps[4][:, 0:2], psb[4], W1[0:64, kv, r, ft * 128:(ft + 1) * 128], posT[0:64, kv, r:r + 2], r == 0, r == 31, (W1b, posTb))
                S.op("vector", lambda e, ft=ft, kv=kv: e.tensor_tensor(out=cb[:, ft:ft + 1], in0=ps[4][:, 0:1], in1=b1[:, kv, ft:ft + 1], op=ALU.add),
                     reads=(psb[4], b1b), writes=(cbb,))
            for g in range(2):
                lo, hi = 64 * g, 64 * g + 64
                for ft in range(2):
                    b = 5 + ft
                    for r in range(32):
                        mm(c, ps[b][:, 0:511], psb[b], W1[lo:hi, kv, r, ft * 128:(ft + 1) * 128], raw[lo:hi, r:r + 8161:16], r == 0, r == 31, (W1b, rawb))
                    hd, hdb = hid[ft]
                    S.op("scalar", lambda e, b=b, hd=hd, ft=ft: e.activation(out=hd[:, 0:511], in_=ps[b][:, 0:511], func=AF.Silu, bias=cb[:, ft:ft + 1], scale=1.0),
                         reads=(psb[b], cbb), writes=(hdb,))
                if kv == 0:
                    for ft in range(2):
                        mm(c, ps[7][:, 0:511], psb[7], W2k[:, ft, :], hid[ft][0][:, 0:511], ft == 0, ft == 1, (W2kb, hid[ft][1]))
                    S.op("vector", lambda e, lo=lo, hi=hi: e.tensor_copy(out=KcT[lo:hi, 0:511], in_=ps[7][lo:hi, 0:511]), reads=(psb[7],), writes=(KcTb,))
                else:
                    for ic in range(4):
                        n_i = 128 if ic < 3 else 127
                        for ft in range(2):
                            mm(c, ps[7][0:n_i, 0:64], psb[7], hid[ft][0][:, ic * 128:ic * 128 + n_i], W2v[:, ft, :], ft == 0, ft == 1, (W2vb, hid[ft][1]))
                        S.op("vector", lambda e, ic=ic, n_i=n_i, g=g: e.tensor_copy(out=Vca[0:n_i, ic, g, 0:64], in_=ps[7][0:n_i, 0:64]), reads=(psb[7],), writes=(Vcab,))
        S.flush()

    if DBG.get("stop") == 1:
        c.top.close()
        return nc
    with contextlib.ExitStack() as ph:
        Wu, Wub = sbt(c, ph, "Wu", [128, 8, 512]); Wv, Wvb = sbt(c, ph, "Wv", [128, 8, 512]); Wq, Wqb = sbt(c, ph, "Wq", [128, 8, 512])
        Wng, Wngb = sbt(c, ph, "Wng", [128, 8, 24])
        brow, browb = sbt(c, ph, "brow", [1, 4, 512], F32)
        lngs, lngb = sbt(c, ph, "lngs", [128, 512], F32); lnbs, lnbb = sbt(c, ph, "lnbs", [128, 512], F32)
        bsb, bsbb = sbt(c, ph, "bsbs", [128, 4, 128], F32)
        Jw, Jwb = sbt(c, ph, "Jw", [40, 272]); I4, I4b = sbt(c, ph, "I4", [128, 512])
        cand, candb = sbt(c, ph, "cand", [128, 256], F32); forc, forcb = sbt(c, ph, "forc", [128, 256], F32)

        for (t_, b_, src) in [(Wu, Wub, w_u), (Wv, Wvb, w_v), (Wq, Wqb, w_q), (Wng, Wngb, w_ng)]:
            load(c, "gpsimd", t_[:], b_, src.rearrange("(kc p) n -> p kc n", p=128))
        load(c, "sync", brow[:, 0, :], browb, b_u); load(c, "sync", brow[:, 1, :], browb, b_v)
        load(c, "sync", brow[:, 2, :], browb, b_q); load(c, "sync", brow[:, 3, 0:24], browb, b_ng)
        load(c, "sync", lngs[:], lngb, lng); load(c, "sync", lnbs[:], lnbb, lnb)
        load(c, "sync", bsb[:], bsbb, bsb_d)
        bqc, bqcb = sbt(c, ph, "bqc", [128, 4], F32)
        load(c, "sync", bqc[:], bqcb, b_qc)
        load(c, "gpsimd", Jw[:], Jwb, jw_d); load(c, "gpsimd", I4[:], I4b, i4_d)
        load(c, "sync", cand[:], candb, cand_d); load(c, "sync", forc[:], forcb, forc_d)
        hTo = [sbt(c, ph, "hTo%d" % i, [128, 8, 128]) for i in range(2)]
        QZ, QZb0 = sbt(c, ph, "QZ", [128, 2, 2, 4, 128])
        QZbs = [QZb0, Buf("QZ1")]
        S.op("vector", lambda e: e.memset(QZ[:], 0.0), writes=(QZbs[0], QZbs[1]))
        xs, xsb = sbt(c, ph, "xs", [128, 512], F32); tt, ttb = sbt(c, ph, "tt", [128, 512], F32)
        gus = [sbt(c, ph, "gu%d" % i, [128, 512], F32) for i in range(2)]; gv, gvb = sbt(c, ph, "gv", [128, 512], F32)
        vlns = [sbt(c, ph, "vln%d" % i, [128, 512]) for i in range(2)]
        aTs = [sbt(c, ph, "aTs%d" % i, [128, 512]) for i in range(2)]
        oTs = [sbt(c, ph, "oTs%d" % i, [128, 512]) for i in range(2)]
        gates = [sbt(c, ph, "gate%d" % i, [128, 24], F32) for i in range(2)]
        osb, osbb = sbt(c, ph, "osb", [128, 512], F32)
        PT = [sbt(c, ph, "PT%d" % i, [128, 512]) for i in range(3)]
        negr = [sbt(c, ph, "negx%d" % i, [128, 2048]) for i in range(2)]
        imp, impb = sbt(c, ph, "imp", [128, 128], F32); imp2, imp2b = sbt(c, ph, "imp2", [128, 128], F32)
        m8a, m8ab = sbt(c, ph, "m8a", [128, 8], F32); m8b, m8bb = sbt(c, ph, "m8b", [128, 8], F32)
        selm, selmb = sbt(c, ph, "selm", [128, 128], F32); nmk, nmkb = sbt(c, ph, "nmk", [128, 128])
        rden, rdenb = sbt(c, ph, "rden", [128, 4], F32); wgt, wgtb = sbt(c, ph, "wgt", [128, 4], F32)
        ptc = [0]
        sc = [0]
        nxc = [0]

        def next_S():
            b = sc[0] % 3
            sc[0] += 1
            return b

        def exp_pv(bS, nk, Obanks, ocol, ow, vfn, vbuf, last, mT=None, mTb=None):
            pi = ptc[0] % 3
            ptc[0] += 1
            P_, Pb_ = PT[pi]
            S.op("scalar", lambda e, bS=bS, nk=nk, P_=P_: e.activation(out=P_[0:nk, :], in_=ps[bS][0:nk, :], func=AF.Exp), reads=(psb[bS],), writes=(Pb_,))
            if mT is not None:
                S.op("vector", lambda e, P_=P_, mT=mT: e.tensor_tensor(out=P_[:, :].rearrange("p (h q) -> p h q", h=4), in0=P_[:, :].rearrange("p (h q) -> p h q", h=4),
                                                                     in1=mT.unsqueeze(1).to_broadcast([128, 4, 128]), op=ALU.mult),
                     reads=(Pb_, mTb), writes=(Pb_,))
            vap = vfn(nk)

            def do_pv():
                for hp in range(4):
                    ob = Obanks[hp]
                    mm(c, ps[ob][:, ocol[hp]:ocol[hp] + ow], psb[ob], P_[0:nk, hp * 128:(hp + 1) * 128], vap, False, last, (Pb_, vbuf))
            pend.append(do_pv)
            while len(pend) > 2:
                pend.pop(0)()

        pend = []

        def flush_pv():
            while pend:
                pend.pop(0)()

        def zero_start(ob, width):
            mm(c, ps[ob][:, 0:width], psb[ob], zer[0:1, 0:128], zer[0:1, 0:width], True, False, (zerb,))

        def finalize(Obanks, ocol, g, br, first):
            for hp in range(4):
                ob = Obanks[hp]
                S.op("vector", lambda e, ob=ob, hp=hp, cc_=ocol[hp]: e.tensor_scalar_max(out=rden[:, hp:hp + 1], in0=ps[ob][:, cc_ + 64:cc_ + 65], scalar1=1e-20),
                     reads=(psb[ob],), writes=(rdenb,))
            S.op("vector", lambda e: e.reciprocal(out=rden[:], in_=rden[:]), reads=(rdenb,), writes=(rdenb,))
            gsl = gate[:, g * 12 + br:g * 12 + br + 10:3]
            S.op("vector", lambda e, gsl=gsl: e.tensor_tensor(out=wgt[:], in0=rden[:], in1=gsl, op=ALU.mult), reads=(rdenb, gateb), writes=(wgtb,))
            for hp in range(4):
                ob = Obanks[hp]
                od = osb[:, (g * 4 + hp) * 64:(g * 4 + hp) * 64 + 64]
                src = ps[ob][:, ocol[hp]:ocol[hp] + 64]
                if first:
                    S.op("vector", lambda e, od=od, src=src, hp=hp: e.tensor_scalar(out=od, in0=src, scalar1=wgt[:, hp:hp + 1], scalar2=None, op0=ALU.mult),
                         reads=(psb[ob], wgtb), writes=(osbb,))
                else:
                    S.op("vector", lambda e, od=od, src=src, hp=hp: e.scalar_tensor_tensor(out=od, in0=src, scalar=wgt[:, hp:hp + 1], in1=od, op0=ALU.mult, op1=ALU.add),
                         reads=(psb[ob], wgtb, osbb), writes=(osbb,))

        lnv_em = {}

        def proj_A(s):
            par = s % 2
            ht, htb = hTo[par]
            load(c, "gpsimd", ht[:], htb, hT_own[:, s * 128:(s + 1) * 128].rearrange("(kc p) t -> p kc t", p=128))
            for m in range(4):
                for kc in range(8):
                    mm(c, ps[7][:, m * 128:(m + 1) * 128], psb[7], Wu[:, kc, m * 128:(m + 1) * 128], ht[:, kc, :], kc == 0, False, (Wub, htb))
                mm(c, ps[7][:, m * 128:(m + 1) * 128], psb[7], brow[0:1, 0, m * 128:(m + 1) * 128], ones[0:1, 0:128], False, True, (browb, onesb))
            for kc in range(8):
                mm(c, ps[0][:, :], psb[0], ht[:, kc, :], Wv[:, kc, :], kc == 0, False, (Wvb, htb))
            mm(c, ps[0][:, :], psb[0], ones[0:1, 0:128], brow[0:1, 1, :], False, True, (browb, onesb))
            for m in range(4):
                for kc in range(8):
                    mm(c, ps[1][:, m * 128:(m + 1) * 128], psb[1], Wq[:, kc, m * 128:(m + 1) * 128], ht[:, kc, :], kc == 0, False, (Wqb, htb))
                mm(c, ps[1][:, m * 128:(m + 1) * 128], psb[1], brow[0:1, 2, m * 128:(m + 1) * 128], ones[0:1, 0:128], False, True, (browb, onesb))
            for kc in range(8):
                mm(c, ps[2][:, 0:24], psb[2], ht[:, kc, :], Wng[:, kc, :], kc == 0, False, (Wngb, htb))
            mm(c, ps[2][:, 0:24], psb[2], ones[0:1, 0:128], brow[0:1, 3, 0:24], False, True, (browb, onesb))
            gu_, gub_ = gus[par]
            vl_, vlb_ = vlns[par]
            S.op("scalar", lambda e: e.activation(out=xs[:], in_=ps[7][:, :], func=AF.Identity), reads=(psb[7],), writes=(xsb,))
            gelu_chain(c, xs[:], xsb, tt[:], ttb, gu_[:], gub_)
            S.op("scalar", lambda e: e.activation(out=xs[:], in_=ps[0][:, :], func=AF.Identity), reads=(psb[0],), writes=(xsb,))
            gelu_chain(c, xs[:], xsb, tt[:], ttb, gv[:], gvb)
            if par not in lnv_em:
                lnv_em[par] = layer_norm_rows(c, ph, gv, gvb, 1, lngs, lngb, lnbs, lnbb, vl_[:], vlb_, "v%d" % par)
            lnv_em[par]()
            for g in range(2):
                S.op("vector", lambda e, g=g, par=par: e.tensor_scalar(out=QZ[64 * g:64 * g + 64, par, g, :, :].rearrange("p a b -> p (a b)"), in0=ps[1][64 * g:64 * g + 64, :],
                                                                     scalar1=0.125, scalar2=None, op0=ALU.mult),
                     reads=(psb[1],), writes=(QZbs[par],))
            gt_, gtb_ = gates[par]
            S.op("scalar", lambda e, gt_=gt_: e.activation(out=gt_[:], in_=ps[2][:, 0:24], func=AF.Sigmoid), reads=(psb[2],), writes=(gtb_,))

        def proj_B(s):
            par = s % 2
            gu_, gub_ = gus[par]
            vl_, vlb_ = vlns[par]
            for g8 in range(8):
                m, half = g8 // 2, g8 % 2
                mm(c, ps[7][64 * half:64 * half + 64, m * 128:(m + 1) * 128], psb[7], vl_[:, g8 * 64:(g8 + 1) * 64], WsT[:, g8, :], True, True, (vlb_, WsTb))
            S.op("vector", lambda e: e.tensor_tensor(out=xs[:], in0=ps[7][:, :], in1=bsb[:].rearrange("p a b -> p (a b)"), op=ALU.add),
                 reads=(psb[7], bsbb), writes=(xsb,))
            at_, atb_ = aTs[par]
            S.op("vector", lambda e, at_=at_, gu_=gu_: e.tensor_tensor(out=at_[:], in0=xs[:], in1=gu_[:], op=ALU.mult),
                 reads=(xsb, gub_), writes=(atb_,))
            S.dma("sync", lambda e, at_=at_, s=s: e.dma_start(out=aT_d[s], in_=at_[:]), reads=(atb_,), writes=(aTdb,))

        nb2 = DBG.get("nb2", NB)
        proj_A(0)
        for s in range(nb2):
            proj_B(s)
            if s + 1 < nb2:
                proj_A(s + 1)
            gate, gateb = gates[s % 2]
            QZb = QZbs[s % 2]
            par_s = s % 2

            for g in range(DBG.get("ng", 2)):
                Qg = QZ[:, par_s, g, :, :].rearrange("p a b -> p (a b)")
                ncmp = 32 * s + 31
                nch = (ncmp + 127) // 128
                Ob = [5, 5, 6, 6]; oc = [0, 193, 0, 193]
                zero_start(5, 386); zero_start(6, 386)
                for cc in range(nch):
                    nk = min(128, ncmp - 128 * cc)
                    bS = next_S()
                    shift = 137 + 128 * cc - 32 * s
                    near = (128 - shift + 39 >= 0) and (128 - shift < nk) and shift >= 0
                    mm(c, ps[bS][0:nk, :], psb[bS], KcT[:, 128 * cc:128 * cc + nk], Qg, True, not near, (KcTb, QZb))
                    if near:
                        mm(c, ps[bS][0:nk, :], psb[bS], Jw[0:40, shift:shift + nk], NCm[0:40, g, :], False, True, (Jwb, NCmb))
                    exp_pv(bS, nk, Ob, oc, 193, lambda nk, cc=cc, g=g: Vca[0:nk, cc, g, :], Vcab, cc == nch - 1)
                flush_pv()
                finalize(Ob, oc, g, 0, True)
                use_mask = s >= 2
                if use_mask:
                    for hp in range(4):
                        src = ps[Ob[hp]][:, oc[hp] + 65:oc[hp] + 193]
                        if hp == 0:
                            S.op("vector", lambda e, src=src: e.tensor_scalar(out=imp[:], in0=src, scalar1=rden[:, 0:1], scalar2=None, op0=ALU.mult),
                                 reads=(psb[Ob[hp]], rdenb), writes=(impb,))
                        else:
                            S.op("vector", lambda e, src=src, hp=hp: e.scalar_tensor_tensor(out=imp[:], in0=src, scalar=rden[:, hp:hp + 1], in1=imp[:], op0=ALU.mult, op1=ALU.add),
                                 reads=(psb[Ob[hp]], rdenb, impb), writes=(impb,))
                    csl = cand[:, 128 - 8 * s:256 - 8 * s]
                    fsl = forc[:, 128 - 8 * s:256 - 8 * s]
                    S.op("vector", lambda e, csl=csl: e.tensor_tensor(out=imp[:], in0=imp[:], in1=csl, op=ALU.mult), reads=(impb, candb), writes=(impb,))
                    S.op("vector", lambda e: e.max(out=m8a[:], in_=imp[:]), reads=(impb,), writes=(m8ab,))
                    S.op("vector", lambda e: e.match_replace(out=imp2[:], in_to_replace=m8a[:], in_values=imp[:], imm_value=-1.0), reads=(impb, m8ab), writes=(imp2b,))
                    S.op("vector", lambda e: e.max(out=m8b[:], in_=imp2[:]), reads=(imp2b,), writes=(m8bb,))
                    S.op("vector", lambda e: e.tensor_scalar(out=selm[:], in0=imp[:], scalar1=m8b[:, 4:5], scalar2=None, op0=ALU.is_ge), reads=(impb, m8bb), writes=(selmb,))
                    S.op("vector", lambda e, fsl=fsl: e.tensor_tensor(out=selm[:], in0=selm[:], in1=fsl, op=ALU.max), reads=(selmb, forcb), writes=(selmb,))
                    S.op("vector", lambda e: e.tensor_copy(out=nmk[:], in_=selm[:]), reads=(selmb,), writes=(nmkb,))
                    S.op("vector", lambda e: e.memset(nmk[:, 0:1], 1.0), reads=(), writes=(nmkb,))
                Ob = [4, 4, 4, 4]; oc = [0, 65, 130, 195]
                zero_start(4, 260)
                wl = [e_ for e_ in range(-4, 4) if 4 * s + e_ >= 0]
                for e_ in wl:
                    ck = 4 * s + e_
                    bS = next_S()
                    mm(c, ps[bS][:, :], psb[bS], KwT[:, 128 * ck:128 * ck + 128], Qg, True, False, (KwTb, QZb))
                    mm(c, ps[bS][:, :], psb[bS], I4[:, 0:128], Mwin[:, g, e_ + 4, :], False, True, (I4b, Mwinb))
                    exp_pv(bS, 128, Ob, oc, 65, lambda nk, ck=ck, g=g: Vw[:, ck, g, :], Vwb, e_ == wl[-1])
                flush_pv()
                finalize(Ob, oc, g, 2, False)
                Ob = [3, 3, 3, 3]; oc = [0, 65, 130, 195]
                zero_start(3, 260)
                nsc = 4 * s + 4
                for ck in range(nsc):
                    bS = next_S()
                    e_ = ck - 4 * s
                    extra = (1 if e_ >= -1 else 0)
                    mm(c, ps[bS][:, :], psb[bS], KsT[:, 128 * ck:128 * ck + 128], Qg, True, extra == 0, (KsTb, QZb))
                    mT_, mTb_ = None, None
                    if use_mask:
                        if ck % 16 == 0:
                            nx, nxb = negr[nxc[0] % 2]
                            nxc[0] += 1
                            nb_ = min(32, 2 * nsc - 2 * ck)
                            S.op("gpsimd", lambda e, nx=nx, nb_=nb_, ck=ck: e.tensor_copy(out=nx[:, 0:nb_ * 64].rearrange("p (a b) -> p a b", b=64),
                                                                                        in_=nmk[:, 2 * ck:2 * ck + nb_].unsqueeze(2).to_broadcast([128, nb_, 64])),
                                 reads=(nmkb,), writes=(nxb,))
                        cl = ck % 16
                        tb_ = 5 + (ck % 2)
                        mT_ = ps[tb_][:, 0:64].bitcast(BF16)
                        mTb_ = psb[tb_]
                        S.op("tensor", lambda e, mT_=mT_, nx=nx, cl=cl: e.transpose(out=mT_, in_=nx[:, 128 * cl:128 * cl + 128], identity=I4[:, 0:128]),
                             reads=(nxb, I4b), writes=(mTb_,))
                    if e_ >= -1:
                        mm(c, ps[bS][:, :], psb[bS], I4[:, 0:128], Msel[:, g, e_ + 1, :], False, True, (I4b, Mselb))
                    exp_pv(bS, 128, Ob, oc, 65, lambda nk, ck=ck, g=g: Vs[:, ck, g, :], Vsb, ck == nsc - 1, mT_, mTb_)
                flush_pv()
                finalize(Ob, oc, g, 1, False)
            for m in range(4):
                S.op("tensor", lambda e, m=m: e.transpose(out=ps[7][:, m * 128:(m + 1) * 128], in_=osb[:, m * 128:(m + 1) * 128], identity=identf[:]),
                     reads=(osbb, identb), writes=(psb[7],))
            ot_, otb_ = oTs[s % 2]
            S.op("scalar", lambda e, ot_=ot_: e.activation(out=ot_[:], in_=ps[7][:, :], func=AF.Identity),
                 reads=(psb[7],), writes=(otb_,))
            S.dma("sync", lambda e, ot_=ot_, s=s: e.dma_start(out=oT_d[s], in_=ot_[:]), reads=(otb_,), writes=(oTdb,))
        S.flush()
    if DBG.get("stop") == 3:
        c.top.close()
        return nc

    kvs.close()
    with contextlib.ExitStack() as ph:
        Wmg, Wmgb = sbt(c, ph, "Wmg", [128, 8, 2048]); Wpa, Wpab = sbt(c, ph, "Wpa", [128, 4, D]); Wpb, Wpbb = sbt(c, ph, "Wpb", [128, 4, D])
        Wo, Wob = sbt(c, ph, "Wo", [128, 8, D]); bmg, bmgb = sbt(c, ph, "bmg", [1, 2048], F32)
        g1, g1b = sbt(c, ph, "g1", [128, D], F32); b1_, b1b_ = sbt(c, ph, "b1_", [128, D], F32)
        hTo = [sbt(c, ph, "hTp%d" % i, [128, 8, 128]) for i in range(2)]
        hb = [sbt(c, ph, "hb%d" % i, [128, D], F32) for i in range(2)]
        aTl = [sbt(c, ph, "aTl%d" % i, [128, 4, 128]) for i in range(2)]
        oTl = [sbt(c, ph, "oTl%d" % i, [128, 4, 128]) for i in range(2)]
        sga, sgab = sbt(c, ph, "sga", [128, 512], F32); sgb, sgbb = sbt(c, ph, "sgb", [128, 512], F32)
        y1, y1b = sbt(c, ph, "y1", [128, 512], F32)
        yT, yTb = sbt(c, ph, "yT", [128, 8, 128])
        rr, rrb = sbt(c, ph, "rr", [128, D], F32)
        ob_ = [sbt(c, ph, "ob%d" % i, [128, D], F32) for i in range(2)]
        def load_blk3(s):
            ht, htb = hTo[s % 2]
            hbt, hbb = hb[s % 2]
            load(c, "gpsimd", ht[:], htb, hT_own[:, s * 128:(s + 1) * 128].rearrange("(kc p) t -> p kc t", p=128))
            load(c, "sync", hbt[:], hbb, h_own[s * 128:(s + 1) * 128, :])
            aT, aTb = aTl[s % 2]; oT, oTb = oTl[s % 2]
            S.dma("sync", lambda e, aT=aT, s=s: e.dma_start(out=aT[:].rearrange("p a b -> p (a b)"), in_=aT_d[s]), reads=(aTdb,), writes=(aTb,))
            S.dma("sync", lambda e, oT=oT, s=s: e.dma_start(out=oT[:].rearrange("p a b -> p (a b)"), in_=oT_d[s]), reads=(oTdb,), writes=(oTb,))
        bmgc, bmgcb = sbt(c, ph, "bmgc", [128, 16], F32)
        load(c, "sync", bmgc[:], bmgcb, b_mgc)
        load_blk3(0)
        Wmgq = [Buf("Wmgq%d" % i) for i in range(4)]
        wmg_v = w_mg.rearrange("(kc p) n -> p kc n", p=128)
        for q4 in (0, 2):
            load(c, "gpsimd", Wmg[:, :, q4 * 512:(q4 + 1) * 512], Wmgq[q4], wmg_v[:, :, q4 * 512:(q4 + 1) * 512])
        load(c, "gpsimd", Wpa[:], Wpab, w_pa.rearrange("(kc p) n -> p kc n", p=128))
        load(c, "gpsimd", Wpb[:], Wpbb, w_pb.rearrange("(kc p) n -> p kc n", p=128))
        for q4 in (1, 3):
            load(c, "gpsimd", Wmg[:, :, q4 * 512:(q4 + 1) * 512], Wmgq[q4], wmg_v[:, :, q4 * 512:(q4 + 1) * 512])
        load(c, "gpsimd", Wo[:], Wob, w_o.rearrange("(kc p) n -> p kc n", p=128))
        load(c, "sync", g1[:], g1b, ln1g); load(c, "sync", b1_[:], b1b_, ln1b)
        ln1 = None
        for s in range(NB):
            ht, htb = hTo[s % 2]
            hbt, hbb = hb[s % 2]
            aT, aTb = aTl[s % 2]; oT, oTb = oTl[s % 2]
            if s + 1 < NB:
                load_blk3(s + 1)
            for half in range(2):
                for (bk, coff, sg_, sgb_) in [(0, half * 512, sga, sgab), (1, 1024 + half * 512, sgb, sgbb)]:
                    for m in range(4):
                        cs = coff + m * 128
                        for kc in range(8):
                            mm(c, ps[bk][:, m * 128:(m + 1) * 128], psb[bk], Wmg[:, kc, cs:cs + 128], ht[:, kc, :], kc == 0, kc == 7, (Wmgq[cs // 512], htb))
                    for m in range(4):
                        ti = (coff + m * 128) // 128
                        S.op("scalar", lambda e, bk=bk, sg_=sg_, m=m, ti=ti: e.activation(out=sg_[:, m * 128:(m + 1) * 128], in_=ps[bk][:, m * 128:(m + 1) * 128], func=AF.Sigmoid,
                                                                                    bias=bmgc[:, ti:ti + 1], scale=1.0),
                             reads=(psb[bk], bmgcb), writes=(sgb_,))
                for (bk, W_, Wb_, xT, xTb) in [(2, Wpa, Wpab, aT, aTb), (3, Wpb, Wpbb, oT, oTb)]:
                    for m in range(4):
                        cs = half * 512 + m * 128
                        for k4 in range(4):
                            mm(c, ps[bk][:, m * 128:(m + 1) * 128], psb[bk], W_[:, k4, cs:cs + 128], xT[:, k4, :], k4 == 0, k4 == 3, (Wb_, xTb))
                S.op("vector", lambda e: e.tensor_tensor(out=y1[:], in0=sga[:], in1=ps[2][:, :], op=ALU.mult), reads=(sgab, psb[2]), writes=(y1b,))
                S.op("vector", lambda e: e.tensor_tensor(out=sgb[:], in0=sgb[:], in1=ps[3][:, :], op=ALU.mult), reads=(sgbb, psb[3]), writes=(sgbb,))
                S.op("vector", lambda e, half=half: e.tensor_tensor(out=yT[:, half * 4:half * 4 + 4, :].rearrange("p a b -> p (a b)"), in0=y1[:], in1=sgb[:], op=ALU.add),
                     reads=(y1b, sgbb), writes=(yTb,))
            for half in range(2):
                bk = 4 + half
                for m in range(8):
                    mm(c, ps[bk][:, :], psb[bk], yT[:, m, :], Wo[:, m, half * 512:(half + 1) * 512], m == 0, m == 7, (yTb, Wob))
                S.op("vector", lambda e, half=half, bk=bk, hbt=hbt: e.scalar_tensor_tensor(out=rr[:, half * 512:(half + 1) * 512], in0=hbt[:, half * 512:(half + 1) * 512],
                                                                                         scalar=ALPHA, in1=ps[bk][:, :], op0=ALU.mult, op1=ALU.add),
                     reads=(hbb, psb[bk]), writes=(rrb,))
            ot, otb = ob_[s % 2]
            if ln1 is None:
                ln1 = {}
            key = s % 2
            if key not in ln1:
                ln1[key] = layer_norm_rows(c, ph, rr, rrb, 2, g1, g1b, b1_, b1b_, ot[:], otb, "1_%d" % key)
            ln1[key]()
            c.S.dma("sync", lambda e, ot=ot, s=s: e.dma_start(out=h1_out[s * 128:(s + 1) * 128, :], in_=ot[:]), reads=(otb,), writes=(outb,))
        S.wait_all_dma("sync")
        S.flush()
    c.top.close()
    return nc


def build_C():
    c = new_ctx()
    nc, S, ps, psb = c.nc, c.S, c.ps, c.psb
    top = c.top
    I = lambda n, s: dram_in(c, n, s)
    h1T = I("h1T", [D, NB, 130]); h1 = I("h1", [2048, D])
    w_up = I("w_up", [D, 2 * DFF]); w_dn = I("w_dn", [DFF, D])
    cw = I("cw", [128, 44, 4]); ln2g = I("ln2g", [128, D]); ln2b = I("ln2b", [128, D])
    h2_out = nc.dram_tensor("h2_out", [2048, D], F32, kind="ExternalOutput").ap()
    outb = Buf("h2_out")
    with contextlib.ExitStack() as ph:
        Wup, Wupb = sbt(c, ph, "Wup", [128, 8, 2 * DFF]); Wdn, Wdnb = sbt(c, ph, "Wdn", [128, 22, D])
        cws, cwb = sbt(c, ph, "cws", [128, 44, 4], F32)
        g2, g2b = sbt(c, ph, "g2", [128, D], F32); b2, b2b = sbt(c, ph, "b2", [128, D], F32)
        GB = 2
        hT = [sbt(c, ph, "hT%d" % i, [128, 8, GB * 130]) for i in range(2)]
        hb = [sbt(c, ph, "hb%d" % i, [128, GB, D], F32) for i in range(2)]
        cg = [sbt(c, ph, "cg%d" % i, [128, GB, 128], F32) for i in range(2)]
        cv = [sbt(c, ph, "cv%d" % i, [128, GB, 128], F32) for i in range(2)]
        sT, sTb = sbt(c, ph, "sT", [128, 22, GB, 128])
        rr, rrb = sbt(c, ph, "rr", [128, D], F32)
        ob_ = [sbt(c, ph, "ob%d" % i, [128, D], F32) for i in range(2)]
        def load_pair(sp):
            ht, htb = hT[sp % 2]
            hbt, hbb = hb[sp % 2]
            for bi in range(GB):
                s = sp * GB + bi
                load(c, "gpsimd", ht[:, :, bi * 130:(bi + 1) * 130], htb, h1T[:, s, :].rearrange("(kc p) t -> p kc t", p=128))
                load(c, "sync", hbt[:, bi, :], hbb, h1[s * 128:(s + 1) * 128, :])
        load(c, "sync", cws[:], cwb, cw)
        load_pair(0)
        Wupq = [Buf("Wupq%d" % i) for i in range(4)]
        for q4 in (0, 2, 1, 3):
            load(c, "gpsimd", Wup[:, :, q4 * 1408:(q4 + 1) * 1408], Wupq[q4], w_up[:, q4 * 1408:(q4 + 1) * 1408].rearrange("(kc p) n -> p kc n", p=128))
        load(c, "gpsimd", Wdn[:], Wdnb, w_dn.rearrange("(kc p) n -> p kc n", p=128))
        load(c, "sync", g2[:], g2b, ln2g); load(c, "sync", b2[:], b2b, ln2b)
        ln2 = {}

        def conv(bk, ft, dst, dstb):
            pv = ps[bk][:, 0:GB * 130].rearrange("p (b t) -> p b t", b=GB)
            S.op("vector", lambda e: e.tensor_scalar(out=dst[:], in0=pv[:, :, 0:128], scalar1=cws[:, ft, 0:1], scalar2=cws[:, ft, 3:4], op0=ALU.mult, op1=ALU.add),
                 reads=(psb[bk], cwb), writes=(dstb,))
            S.op("vector", lambda e: e.scalar_tensor_tensor(out=dst[:], in0=pv[:, :, 1:129], scalar=cws[:, ft, 1:2], in1=dst[:], op0=ALU.mult, op1=ALU.add),
                 reads=(psb[bk], cwb, dstb), writes=(dstb,))
            S.op("vector", lambda e: e.scalar_tensor_tensor(out=dst[:], in0=pv[:, :, 2:130], scalar=cws[:, ft, 2:3], in1=dst[:], op0=ALU.mult, op1=ALU.add),
                 reads=(psb[bk], cwb, dstb), writes=(dstb,))

        for sp in range(NB // GB):
            ht, htb = hT[sp % 2]
            hbt, hbb = hb[sp % 2]
            for f in range(22):
                bg, bv = (f % 2) * 2, (f % 2) * 2 + 1
                cgt, cgb = cg[f % 2]
                cvt, cvb = cv[f % 2]
                for (bk, ft) in [(bg, f), (bv, 22 + f)]:
                    for kc in range(8):
                        mm(c, ps[bk][:, 0:GB * 130], psb[bk], Wup[:, kc, ft * 128:(ft + 1) * 128], ht[:, kc, :], kc == 0, kc == 7, (Wupq[ft // 11], htb))
                if f == 0 and sp + 1 < NB // GB:
                    load_pair(sp + 1)
                conv(bg, f, cgt, cgb)
                conv(bv, 22 + f, cvt, cvb)
                S.op("scalar", lambda e, cgt=cgt: e.activation(out=cgt[:], in_=cgt[:], func=AF.Silu), reads=(cgb,), writes=(cgb,))
                S.op("vector", lambda e, f=f, cgt=cgt, cvt=cvt: e.tensor_tensor(out=sT[:, f, :, :], in0=cgt[:], in1=cvt[:], op=ALU.mult), reads=(cgb, cvb), writes=(sTb,))
            for bi in range(GB):
                s = sp * GB + bi
                for half in range(2):
                    bk = 4 + half
                    for f in range(22):
                        mm(c, ps[bk][:, :], psb[bk], sT[:, f, bi, :], Wdn[:, f, half * 512:(half + 1) * 512], f == 0, f == 21, (sTb, Wdnb))
                    S.op("vector", lambda e, half=half, bk=bk, hbt=hbt, bi=bi: e.scalar_tensor_tensor(out=rr[:, half * 512:(half + 1) * 512], in0=hbt[:, bi, half * 512:(half + 1) * 512],
                                                                                                 scalar=ALPHA, in1=ps[bk][:, :], op0=ALU.mult, op1=ALU.add),
                         reads=(hbb, psb[bk]), writes=(rrb,))
                ot, otb = ob_[s % 2]
                key = s % 2
                if key not in ln2:
                    ln2[key] = layer_norm_rows(c, ph, rr, rrb, 2, g2, g2b, b2, b2b, ot[:], otb, "2_%d" % key)
                ln2[key]()
                c.S.dma("sync", lambda e, ot=ot, s=s: e.dma_start(out=h2_out[s * 128:(s + 1) * 128, :], in_=ot[:]), reads=(otb,), writes=(outb,))
        S.wait_all_dma("sync")
        S.flush()
    c.top.close()
    return nc


def t5_bucket_np(dist):
    n = np.maximum(dist, 0)
    max_exact = 16
    lr = np.log(np.maximum(n, 1).astype(np.float32) / np.float32(max_exact)) / np.float32(math.log(128 / 16))
    large = np.minimum(max_exact + (lr.astype(np.float32) * np.float32(16)).astype(np.int32), 31)
    return np.where(n < max_exact, n, large).astype(np.int64)


def own_index(j):
    return np.concatenate([128 * (4 * s + j) + np.arange(128) for s in range(NB)])


_CONST = {}


def core_consts(j):
    if j in _CONST:
        return _CONST[j]
    k = np.arange(128)[:, None]
    q = np.arange(128)[None, :]
    cs = {}
    dsel = [128 * (j - e) + q - k for e in range(-1, 4)]
    dwin = [128 * (j - e) + q - k for e in range(-4, 4)]
    y = np.arange(40)[:, None]
    dcmp = 128 * j + q - 16 * y + 113
    cs["isel"] = [t5_bucket_np(d) for d in dsel]
    cs["iwin"] = [t5_bucket_np(d) for d in dwin]
    cs["icmp"] = t5_bucket_np(dcmp)
    f32 = np.float32
    cs["nsel"] = np.stack([np.tile(np.where(d >= 0, 0.0, NEG).astype(f32), (1, 4)) for d in dsel])
    cs["nwin"] = np.stack([np.tile(np.where((d >= 0) & (d < 512), 0.0, NEG).astype(f32), (1, 4)) for d in dwin])
    cs["ncmp"] = np.tile(np.where(dcmp >= 0, 0.0, NEG).astype(f32), (1, 4))
    x = np.arange(256)[None, :] - 2 * j
    qq = np.arange(128)[:, None]
    hi = (qq >= 64).astype(np.int64)
    cs["cand"] = (x <= 126 + hi).astype(f32)
    cs["forc"] = ((x == 127 + hi) | (x == 128 + hi)).astype(f32)
    _CONST[j] = cs
    return cs


def shared_consts():
    if "sh" in _CONST:
        return _CONST["sh"]
    f32 = np.float32
    sh = {}
    jw = np.zeros((40, 272), f32)
    jw[np.arange(40), np.arange(40) + 128] = 1.0
    sh["jw"] = jw
    sh["i4"] = np.tile(np.eye(128, dtype=f32), (1, 4))
    sh["triu"] = np.triu(np.ones((128, 128), f32))
    n_cmp = SEQ // 16 - 1
    cstart = np.arange(512)[:, None] * 16
    sstart = np.arange(128)[None, :] * 64
    ov = np.minimum(cstart + 32, sstart + 64) - np.maximum(cstart, sstart)
    ov = (np.maximum(ov, 0) / 32).astype(f32)
    ov[n_cmp:, :] = 0.0
    ov[:, 0] = 0.0
    sh["ovl"] = np.ascontiguousarray(ov.reshape(4, 128, 128).transpose(1, 0, 2))
    _CONST["sh"] = sh
    return sh


def rep128(v):
    return np.ascontiguousarray(np.broadcast_to(np.asarray(v, np.float32)[None, :], (128, v.shape[0])))


def layer_inputs_B(l, P):
    f32 = np.float32
    w_in = P["w_in"][l]; b_in = P["b_in"][l]
    o_u, o_v, o_q, o_kc, o_vc, o_ks, o_vs, o_kw, o_vw, o_ng, o_mg = 0, 512, 1024, 1536, 1664, 1792, 1920, 2048, 2176, 2304, 2328
    fcols = np.concatenate([np.arange(o_kc, o_kc + 128), np.arange(o_vc, o_vc + 128), np.arange(o_ks, o_ks + 128), np.arange(o_kw, o_kw + 128)])
    tcols = np.concatenate([np.arange(o_vs, o_vs + 128), np.arange(o_vw, o_vw + 128)])
    qcols = np.concatenate([np.concatenate([o_q + hp * 64 + np.arange(64), o_q + (4 + hp) * 64 + np.arange(64)]) for hp in range(4)])
    C = np.ascontiguousarray
    sh = shared_consts()
    d = {}
    d["w_kvf"] = C(w_in[:, fcols]); d["w_kvt"] = C(w_in[:, tcols]); d["b_kvf"] = C(b_in[fcols][None]); d["b_kvt"] = C(b_in[tcols][None]); d["b_kvfc"] = C(b_in[fcols].reshape(4, 128).T)
    d["w_u"] = C(w_in[:, o_u:o_u + 512]); d["w_v"] = C(w_in[:, o_v:o_v + 512]); d["w_q"] = C(w_in[:, qcols]); d["w_ng"] = C(w_in[:, o_ng:o_ng + 24])
    d["b_u"] = C(b_in[None, o_u:o_u + 512]); d["b_v"] = C(b_in[None, o_v:o_v + 512]); d["b_q"] = C(b_in[qcols][None]); d["b_ng"] = C(b_in[None, o_ng:o_ng + 24])
    d["b_qc"] = C(b_in[o_q:o_q + 512].reshape(2, 4, 64).transpose(0, 2, 1).reshape(128, 4))
    d["w_mg"] = C(w_in[:, o_mg:o_mg + 2048]); d["b_mg"] = C(b_in[None, o_mg:o_mg + 2048]); d["b_mgc"] = C(b_in[o_mg:o_mg + 2048].reshape(16, 128).T)
    d["lng"] = rep128(P["gm_ln_g"][l]); d["lnb"] = rep128(P["gm_ln_b"][l])
    d["wsT"] = C(P["gm_ws"][l].transpose(2, 0, 1))
    d["triu"] = sh["triu"]
    bs = P["gm_bs"][l]
    d["bsb"] = C(np.stack([np.concatenate([np.broadcast_to(bs[2 * m][None], (64, 128)), np.broadcast_to(bs[2 * m + 1][None], (64, 128))], 0) for m in range(4)], 1))
    w1 = P["cmp_w1"][l].reshape(2, 32, 64, 256).transpose(2, 0, 1, 3)
    d["w1d"] = C(np.concatenate([w1, w1], 0))
    pT = np.zeros((64, 2, 34), f32)
    pT[:, :, :32] = P["cmp_pos"][l].transpose(2, 0, 1)
    d["posT"] = pT
    d["b1"] = C(P["cmp_b1"][l].reshape(2, 2, 128).transpose(2, 0, 1))
    w2k = P["cmp_w2"][l][0].reshape(2, 128, 64).transpose(1, 0, 2)
    d["w2k"] = C(np.concatenate([w2k, w2k], 2))
    d["w2v"] = C(P["cmp_w2"][l][1].reshape(2, 128, 64).transpose(1, 0, 2))
    d["ovl"] = sh["ovl"]
    d["w_pa"] = C(P["w_proj_a"][l]); d["w_pb"] = C(P["w_proj_b"][l]); d["w_o"] = C(P["w_out"][l])
    d["ln1g"] = rep128(P["ln1_g"][l]); d["ln1b"] = rep128(P["ln1_b"][l])
    d["jw"] = sh["jw"]; d["i4"] = sh["i4"]
    rb = P["rel_bias"]
    d["far"] = C(np.stack([np.broadcast_to(np.repeat(rb[31, 4 * g:4 * g + 4], 128)[None], (128, 512)) for g in range(2)]))
    return d


def core_inputs_B(j, P):
    cs = core_consts(j)
    rb = P["rel_bias"]
    C = np.ascontiguousarray

    def gath(idx, g):
        return np.concatenate([rb[idx, 4 * g + hp] for hp in range(4)], axis=1)
    d = {}
    d["bsel"] = C(np.stack([np.stack([gath(ix, g) for ix in cs["isel"]]) for g in range(2)]))
    d["bwin"] = C(np.stack([np.stack([gath(ix, g) for ix in cs["iwin"]]) for g in range(2)]))
    d["bcmp"] = C(np.stack([gath(cs["icmp"], g) for g in range(2)]))
    d["nsel"] = cs["nsel"]; d["nwin"] = cs["nwin"]; d["ncmp"] = cs["ncmp"]
    d["cand"] = cs["cand"]; d["forc"] = cs["forc"]
    return d


_PROG = {}


def get_prog(name):
    if name not in _PROG:
        _PROG[name] = build_B() if name == "B" else build_C()
    return _PROG[name]


def run_B(l, h, P):
    nc = get_prog("B")
    wl = layer_inputs_B(l, P)
    in_maps = []
    for c in range(8):
        b, j = c // 4, c % 4
        idx = own_index(j)
        d = dict(wl)
        d.update(core_inputs_B(j, P))
        d["hT_full"] = np.ascontiguousarray(h[b].T)
        d["hT_own"] = np.ascontiguousarray(h[b][idx].T)
        d["h_own"] = np.ascontiguousarray(h[b][idx])
        in_maps.append(d)
    res = run_bass_kernel_spmd(nc, in_maps, core_ids=list(range(8)))
    out = np.empty_like(h)
    for c in range(8):
        b, j = c // 4, c % 4
        out[b][own_index(j)] = res.results[c]["h1_out"]
    return out


def run_C(l, h1, P):
    nc = get_prog("C")
    C = np.ascontiguousarray
    cwt = np.stack([P["ffn_conv_w"][l][0], P["ffn_conv_w"][l][1], P["ffn_conv_w"][l][2], P["ffn_conv_b"][l]], 1)
    wl = {"w_up": C(P["ffn_w_up"][l]), "w_dn": C(P["ffn_w_down"][l]), "cw": C(cwt.reshape(44, 128, 4).transpose(1, 0, 2)),
          "ln2g": rep128(P["ln2_g"][l]), "ln2b": rep128(P["ln2_b"][l])}
    in_maps = []
    for c in range(8):
        b, j = c // 4, c % 4
        idx = own_index(j)
        hp = np.concatenate([np.zeros((2, D), np.float32), h1[b]], 0)
        blocks = np.stack([hp[128 * (4 * s + j):128 * (4 * s + j) + 130] for s in range(NB)])
        d = dict(wl)
        d["h1T"] = C(blocks.transpose(2, 0, 1))
        d["h1"] = C(h1[b][idx])
        in_maps.append(d)
    res = run_bass_kernel_spmd(nc, in_maps, core_ids=list(range(8)))
    out = np.empty_like(h1)
    for c in range(8):
        b, j = c // 4, c % 4
        out[b][own_index(j)] = res.results[c]["h2_out"]
    return out


def kernel(**inputs):
    P = {k: np.asarray(v, dtype=np.float32) for k, v in inputs.items()}
    h = P["x"]
    for l in range(2):
        h1 = run_B(l, h, P)
        h = run_C(l, h1, P)
    return h.astype(np.float32)
```

```python
import math
import contextlib
import numpy as np
import concourse.bass as bass
import concourse.mybir as mybir
from concourse.bass_utils import run_bass_kernel_spmd

F32 = mybir.dt.float32
BF16 = mybir.dt.bfloat16
AF = mybir.ActivationFunctionType
ALU = mybir.AluOpType

D = 1024
SEQ = 8192
NB = 16
DFF = 2816
ALPHA = (2.0 * 2) ** 0.25
EPS = 1e-5
NEG = -30000.0
GC = 1.5957691216057308


class Buf:
    def __init__(self, name):
        self.name = name
        self.w = None
        self.r = {}
        self.dsem = None
        self.dcnt = 0


class Sched:
    ENG = ["sync", "tensor", "vector", "scalar", "gpsimd"]

    def __init__(self, nc, stack):
        self.nc = nc
        self.stack = stack
        self.prog = {e: [] for e in self.ENG}
        self.sem = {e: stack.enter_context(nc.semaphore("sem_" + e)) for e in self.ENG}
        self.cnt = {e: 0 for e in self.ENG}
        self.waited = {e: {} for e in self.ENG}
        self.dma_bufs = []
        self.nsem = 0

    def _waits(self, eng, reads, writes):
        deps = []
        for b in reads:
            if b.w is not None:
                deps.append(b.w)
        for b in writes:
            if b.w is not None:
                deps.append(b.w)
            deps.extend(b.r.values())
        out = {}
        for (sem, val, src) in deps:
            if src == eng and eng == "tensor":
                continue
            k = id(sem)
            if self.waited[eng].get(k, 0) >= val:
                continue
            if k not in out or out[k][1] < val:
                out[k] = (sem, val)
        for k, (sem, val) in out.items():
            self.waited[eng][k] = val
        return list(out.values())

    def op(self, eng, fn, reads=(), writes=()):
        waits = self._waits(eng, reads, writes)
        self.cnt[eng] += 1
        pt = (self.sem[eng], self.cnt[eng], eng)
        self.prog[eng].append((waits, fn, self.sem[eng], 1))
        for b in reads:
            b.r[id(pt[0])] = pt
        for b in writes:
            b.w = pt
            b.r = {}

    def dma(self, q, fn, reads=(), writes=()):
        waits = self._waits(q, reads, writes)
        tgt = writes[0]
        if tgt.dsem is None:
            tgt.dsem = self.stack.enter_context(self.nc.semaphore("dsem_%d" % self.nsem))
            self.nsem += 1
            self.dma_bufs.append(tgt)
        tgt.dcnt += 16
        pt = (tgt.dsem, tgt.dcnt, "dma")
        self.prog[q].append((waits, fn, tgt.dsem, 16))
        for b in reads:
            b.r[id(pt[0])] = pt
        for b in writes:
            b.w = pt
            b.r = {}

    def wait_all_dma(self, eng="sync"):
        ws = []
        for b in self.dma_bufs:
            if b.dcnt > 0 and self.waited[eng].get(id(b.dsem), 0) < b.dcnt:
                ws.append((b.dsem, b.dcnt))
                self.waited[eng][id(b.dsem)] = b.dcnt
        if ws:
            self.prog[eng].append((ws, None, None, 0))

    def fence(self):
        pts = [(self.sem[e], self.cnt[e]) for e in self.ENG if self.cnt[e] > 0]
        for e in self.ENG:
            ws = []
            for (sem, val) in pts:
                if sem is self.sem[e]:
                    continue
                if self.waited[e].get(id(sem), 0) < val:
                    ws.append((sem, val))
                    self.waited[e][id(sem)] = val
            for b in self.dma_bufs:
                if b.dcnt > 0 and self.waited[e].get(id(b.dsem), 0) < b.dcnt:
                    ws.append((b.dsem, b.dcnt))
                    self.waited[e][id(b.dsem)] = b.dcnt
            if ws:
                self.prog[e].append((ws, None, None, 0))

    def flush(self):
        self.fence()
        prog = self.prog
        self.prog = {e: [] for e in self.ENG}
        with self.nc.Block() as block:
            def mk(ename):
                items = prog[ename]

                def body(e):
                    for (waits, fn, sem, amt) in items:
                        for (s, v) in waits:
                            e.wait_ge(s, v)
                        if fn is not None:
                            ins = fn(e)
                            ins.then_inc(sem, amt)
                return body
            block.sync(mk("sync"))
            block.tensor(mk("tensor"))
            block.vector(mk("vector"))
            block.scalar(mk("scalar"))
            block.gpsimd(mk("gpsimd"))


class Ctx:
    pass


def new_ctx():
    nc = bass.Bass("TRN2", target_bir_lowering=False)
    c = Ctx()
    c.nc = nc
    c.top = contextlib.ExitStack()
    c.S = Sched(nc, c.top)
    c.ins = {}
    c.ps = [c.top.enter_context(nc.psum_tensor("ps%d" % i, [128, 512], F32)) for i in range(8)]
    c.psb = [Buf("ps%d" % i) for i in range(8)]
    return c


def dram_in(c, name, shape):
    ap = c.nc.dram_tensor(name, list(shape), F32, kind="ExternalInput").ap()
    c.ins[name] = ap
    return ap


def sbt(c, st, name, shape, d=BF16):
    t = st.enter_context(c.nc.sbuf_tensor("sb_" + name, list(shape), d))
    return t, Buf(name)


def load(c, q, dst_ap, dst_buf, src_ap):
    c.S.dma(q, lambda e: e.dma_start(out=dst_ap, in_=src_ap), reads=(), writes=(dst_buf,))


def mm(c, out_ap, out_buf, lhsT, rhs, start, stop, reads):
    c.S.op("tensor", lambda e: e.matmul(out_ap, lhsT=lhsT, rhs=rhs, start=start, stop=stop, skip_group_check=True),
           reads=reads, writes=(out_buf,))


def gelu_chain(c, x_ap, xb, t_ap, tb, out_ap, outb):
    S = c.S
    S.op("vector", lambda e: e.tensor_tensor(out=t_ap, in0=x_ap, in1=x_ap, op=ALU.mult), reads=(xb,), writes=(tb,))
    S.op("vector", lambda e: e.tensor_scalar(out=t_ap, in0=t_ap, scalar1=0.044715, scalar2=1.0, op0=ALU.mult, op1=ALU.add),
         reads=(tb,), writes=(tb,))
    S.op("vector", lambda e: e.tensor_tensor(out=t_ap, in0=t_ap, in1=x_ap, op=ALU.mult), reads=(tb, xb), writes=(tb,))
    S.op("scalar", lambda e: e.activation(out=t_ap, in_=t_ap, func=AF.Sigmoid, scale=GC), reads=(tb,), writes=(tb,))
    S.op("vector", lambda e: e.tensor_tensor(out=out_ap, in0=x_ap, in1=t_ap, op=ALU.mult), reads=(xb, tb), writes=(outb,))


def layer_norm_rows(c, st, r, rb, nhalf, g_t, gb, b_t, bb, out_ap, outb, tag):
    S = c.S
    stats, sb_ = sbt(c, st, "lnst" + tag, [128, nhalf, 6], F32)
    mv, mvb = sbt(c, st, "lnmv" + tag, [128, 2], F32)
    rs, rsb = sbt(c, st, "lnrs" + tag, [128, 1], F32)

    def emit():
        for hf in range(nhalf):
            S.op("vector", lambda e, hf=hf: e.bn_stats(out=stats[:, hf, :], in_=r[:, hf * 512:(hf + 1) * 512]),
                 reads=(rb,), writes=(sb_,))
        S.op("vector", lambda e: e.bn_aggr(out=mv[:], in_=stats[:].rearrange("p a b -> p (a b)")), reads=(sb_,), writes=(mvb,))
        S.op("scalar", lambda e: e.activation(out=rs[:], in_=mv[:, 1:2], func=AF.Sqrt, bias=EPS, scale=1.0), reads=(mvb,), writes=(rsb,))
        S.op("vector", lambda e: e.reciprocal(out=rs[:], in_=rs[:]), reads=(rsb,), writes=(rsb,))
        S.op("vector", lambda e: e.tensor_scalar(out=r[:], in0=r[:], scalar1=mv[:, 0:1], scalar2=rs[:, 0:1],
                                                 op0=ALU.subtract, op1=ALU.mult), reads=(rb, mvb, rsb), writes=(rb,))
        S.op("vector", lambda e: e.tensor_tensor(out=r[:], in0=r[:], in1=g_t[:], op=ALU.mult), reads=(rb, gb), writes=(rb,))
        S.op("vector", lambda e: e.tensor_tensor(out=out_ap, in0=r[:], in1=b_t[:], op=ALU.add), reads=(rb, bb), writes=(outb,))
    return emit


DBG = {}


def build_B():
    c = new_ctx()
    nc, S, ps, psb = c.nc, c.S, c.ps, c.psb
    top = c.top
    I = lambda n, s: dram_in(c, n, s)
    hT_full = I("hT_full", [D, SEQ]); hT_own = I("hT_own", [D, 2048]); h_own = I("h_own", [2048, D])
    w_kvf = I("w_kvf", [D, 512]); w_kvt = I("w_kvt", [D, 256]); b_kvf = I("b_kvf", [1, 512]); b_kvt = I("b_kvt", [1, 256]); b_kvfc = I("b_kvfc", [128, 4]); b_kvr = I("b_kvr", [128, 256])
    w_u = I("w_u", [D, 512]); w_v = I("w_v", [D, 512]); w_q = I("w_q", [D, 512]); w_ng = I("w_ng", [D, 24])
    b_u = I("b_u", [1, 512]); b_v = I("b_v", [1, 512]); b_q = I("b_q", [1, 512]); b_ng = I("b_ng", [1, 24]); b_qc = I("b_qc", [128, 4])
    w_mg = I("w_mg", [D, 2048]); b_mg = I("b_mg", [1, 2048]); b_mgc = I("b_mgc", [128, 16])
    lng = I("lng", [128, 512]); lnb = I("lnb", [128, 512])
    wsT = I("wsT", [128, 8, 128]); triu = I("triu", [128, 128]); bsb_d = I("bsb", [128, 4, 128])
    w1d = I("w1d", [128, 2, 32, 256]); posT_d = I("posT", [64, 2, 34]); b1_d = I("b1", [128, 2, 2])
    w2k_d = I("w2k", [128, 2, 128]); w2v_d = I("w2v", [128, 2, 64]); ovl_d = I("ovl", [128, 4, 128])
    w_pa = I("w_pa", [512, D]); w_pb = I("w_pb", [512, D]); w_o = I("w_o", [D, D])
    ln1g = I("ln1g", [128, D]); ln1b = I("ln1b", [128, D])
    bsel_d = I("bsel", [2, 5, 128, 512]); bwin_d = I("bwin", [2, 8, 128, 512]); bcmp_d = I("bcmp", [2, 40, 512])
    far_d = I("far", [2, 128, 512]); nsel_d = I("nsel", [5, 128, 512]); nwin_d = I("nwin", [8, 128, 512]); ncmp_d = I("ncmp", [40, 512])
    jw_d = I("jw", [40, 272]); i4_d = I("i4", [128, 512]); cand_d = I("cand", [128, 256]); forc_d = I("forc", [128, 256])
    h1_out = nc.dram_tensor("h1_out", [2048, D], F32, kind="ExternalOutput").ap()
    outb = Buf("h1_out")

    KsT, KsTb = sbt(c, top, "KsT", [128, SEQ]); KwT, KwTb = sbt(c, top, "KwT", [128, SEQ])
    Vs, Vsb = sbt(c, top, "Vs", [128, 64, 2, 65]); Vw, Vwb = sbt(c, top, "Vw", [128, 64, 2, 65])
    KcT, KcTb = sbt(c, top, "KcT", [128, 512]); Vca, Vcab = sbt(c, top, "Vca", [128, 4, 2, 193])
    aT_d = nc.dram_tensor("aT_scr", [NB, 128, 512], BF16, kind="Internal").ap(); aTdb = Buf("aT_scr")
    oT_d = nc.dram_tensor("oT_scr", [NB, 128, 512], BF16, kind="Internal").ap(); oTdb = Buf("oT_scr")
    ones, onesb = sbt(c, top, "ones", [1, 512], F32)
    identf, identb = sbt(c, top, "identf", [128, 128], F32)
    zer, zerb = sbt(c, top, "zer", [1, 512], BF16)
    S.op("vector", lambda e: e.memset(ones[:], 1.0), writes=(onesb,))
    S.op("vector", lambda e: e.memset(zer[:], 0.0), writes=(zerb,))
    load(c, "sync", identf[:], identb, i4_d[:, 0:128])
    S.op("vector", lambda e: e.memset(Vs[:, :, :, 64:65], 1.0), writes=(Vsb,))
    S.op("vector", lambda e: e.memset(Vw[:, :, :, 64:65], 1.0), writes=(Vwb,))
    S.op("vector", lambda e: e.memset(Vca[:, :, :, 64:65], 1.0), writes=(Vcab,))
    for g in range(2):
        load(c, "gpsimd", Vca[:, :, g, 65:193], Vcab, ovl_d)

    with contextlib.ExitStack() as ph:
        Wkvf, Wkvfb = sbt(c, ph, "Wkvf", [128, 8, 512]); Wkvt, Wkvtb = sbt(c, ph, "Wkvt", [128, 8, 256])
        bkvf, bkvfb = sbt(c, ph, "bkvf", [1, 512], F32); bkvt, bkvtb = sbt(c, ph, "bkvt", [1, 256], F32)
        Kcr, Kcrb = sbt(c, ph, "Kcr", [128, SEQ]); Vcr, Vcrb = sbt(c, ph, "Vcr", [128, SEQ])
        hTf = [sbt(c, ph, "hTf%d" % i, [128, 8, 512]) for i in range(2)]
        W1, W1b = sbt(c, ph, "W1", [128, 2, 32, 256]); posT, posTb = sbt(c, ph, "posTs", [64, 2, 34])
        b1, b1b = sbt(c, ph, "b1s", [128, 2, 2], F32); W2k, W2kb = sbt(c, ph, "W2k", [128, 2, 128]); W2v, W2vb = sbt(c, ph, "W2v", [128, 2, 64])
        hid = [sbt(c, ph, "hid%d" % i, [128, 512]) for i in range(2)]
        cb, cbb = sbt(c, ph, "cb", [128, 2], F32)
        load(c, "gpsimd", Wkvf[:], Wkvfb, w_kvf.rearrange("(kc p) n -> p kc n", p=128))
        load(c, "gpsimd", Wkvt[:], Wkvtb, w_kvt.rearrange("(kc p) n -> p kc n", p=128))
        load(c, "sync", bkvf[:], bkvfb, b_kvf); load(c, "sync", bkvt[:], bkvtb, b_kvt)
        bkvfc, bkvfcb = sbt(c, ph, "bkvfc", [128, 4], F32)
        bkvr, bkvrb = sbt(c, ph, "bkvr", [128, 256], F32)
        load(c, "sync", bkvr[:], bkvrb, b_kvr)
        load(c, "sync", bkvfc[:], bkvfcb, b_kvfc)
        load(c, "gpsimd", W1[:], W1b, w1d); load(c, "gpsimd", posT[:], posTb, posT_d); load(c, "sync", b1[:], b1b, b1_d)
        load(c, "gpsimd", W2k[:], W2kb, w2k_d); load(c, "gpsimd", W2v[:], W2vb, w2v_d)
        fdst = [(Kcr, Kcrb), (Vcr, Vcrb), (KsT, KsTb), (KwT, KwTb)]
        for gi in range(16):
            ht, htb = hTf[gi % 2]
            load(c, "gpsimd", ht[:], htb, hT_full[:, gi * 512:(gi + 1) * 512].rearrange("(kc p) t -> p kc t", p=128))
            for ti in range(4):
                b = ti % 2
                for kc in range(8):
                    mm(c, ps[b][:, :], psb[b], Wkvf[:, kc, ti * 128:(ti + 1) * 128], ht[:, kc, :], kc == 0, kc == 7, (Wkvfb, htb))
                dt_, dtb = fdst[ti]
                S.op("scalar", lambda e, b=b, dt_=dt_, gi=gi, ti=ti: e.activation(out=dt_[:, gi * 512:(gi + 1) * 512], in_=ps[b][:, :], func=AF.Identity,
                                                                            bias=bkvfc[:, ti:ti + 1], scale=1.0),
                     reads=(psb[b], bkvfcb), writes=(dtb,))
            for tcn in range(4):
                cidx = 4 * gi + tcn
                b = 2 + tcn % 2
                for kc in range(8):
                    mm(c, ps[b][:, 0:256], psb[b], ht[:, kc, tcn * 128:(tcn + 1) * 128], Wkvt[:, kc, :], kc == 0, kc == 7, (Wkvtb, htb))
                S.op("vector", lambda e, b=b, cidx=cidx: e.tensor_tensor(out=Vs[:, cidx, :, 0:64], in0=ps[b][:, 0:128].rearrange("p (g d) -> p g d", g=2),
                                                                       in1=bkvr[:, 0:128].rearrange("p (g d) -> p g d", g=2), op=ALU.add),
                     reads=(psb[b], bkvrb), writes=(Vsb,))
                S.op("vector", lambda e, b=b, cidx=cidx: e.tensor_tensor(out=Vw[:, cidx, :, 0:64], in0=ps[b][:, 128:256].rearrange("p (g d) -> p g d", g=2),
                                                                       in1=bkvr[:, 128:256].rearrange("p (g d) -> p g d", g=2), op=ALU.add),
                     reads=(psb[b], bkvrb), writes=(Vwb,))
        for kv in range(2):
            raw, rawb = (Kcr, Kcrb) if kv == 0 else (Vcr, Vcrb)
            for ft in range(2):
                for r in range(32):
                    mm(c, ps[4][:, 0:2], psb[4], W1[0:64, kv, r, ft * 128:(ft + 1) * 128], posT[0:64, kv, r:r + 2], r == 0, r == 31, (W1b, posTb))
                S.op("vector", lambda e, ft=ft, kv=kv: e.tensor_tensor(out=cb[:, ft:ft + 1], in0=ps[4][:, 0:1], in1=b1[:, kv, ft:ft + 1], op=ALU.add),
                     reads=(psb[4], b1b), writes=(cbb,))
            for g in range(2):
                lo, hi = 64 * g, 64 * g + 64
                for ft in range(2):
                    b = 5 + ft
                    for r in range(32):
                        mm(c, ps[b][:, 0:511], psb[b], W1[lo:hi, kv, r, ft * 128:(ft + 1) * 128], raw[lo:hi, r:r + 8161:16], r == 0, r == 31, (W1b, rawb))
                    hd, hdb = hid[ft]
                    S.op("scalar", lambda e, b=b, hd=hd, ft=ft: e.activation(out=hd[:, 0:511], in_=ps[b][:, 0:511], func=AF.Silu, bias=cb[:, ft:ft + 1], scale=1.0),
                         reads=(psb[b], cbb), writes=(hdb,))
                if kv == 0:
                    for ft in range(2):
                        mm(c, ps[7][:, 0:511], psb[7], W2k[:, ft, :], hid[ft][0][:, 0:511], ft == 0, ft == 1, (W2kb, hid[ft][1]))
                    S.op("vector", lambda e, lo=lo, hi=hi: e.tensor_copy(out=KcT[lo:hi, 0:511], in_=ps[7][lo:hi, 0:511]), reads=(psb[7],), writes=(KcTb,))
                else:
                    for ic in range(4):
                        n_i = 128 if ic < 3 else 127
                        for ft in range(2):
                            mm(c, ps[7][0:n_i, 0:64], psb[7], hid[ft][0][:, ic * 128:ic * 128 + n_i], W2v[:, ft, :], ft == 0, ft == 1, (W2vb, hid[ft][1]))
                        S.op("vector", lambda e, ic=ic, n_i=n_i, g=g: e.tensor_copy(out=Vca[0:n_i, ic, g, 0:64], in_=ps[7][0:n_i, 0:64]), reads=(psb[7],), writes=(Vcab,))
        S.flush()

    if DBG.get("stop") == 1:
        c.top.close()
        return nc
    with contextlib.ExitStack() as ph:
        Wu, Wub = sbt(c, ph, "Wu", [128, 8, 512]); Wv, Wvb = sbt(c, ph, "Wv", [128, 8, 512]); Wq, Wqb = sbt(c, ph, "Wq", [128, 8, 512])
        Wng, Wngb = sbt(c, ph, "Wng", [128, 8, 24])
        brow, browb = sbt(c, ph, "brow", [1, 4, 512], F32)
        lngs, lngb = sbt(c, ph, "lngs", [128, 512], F32); lnbs, lnbb = sbt(c, ph, "lnbs", [128, 512], F32)
        WsT, WsTb = sbt(c, ph, "WsT", [128, 8, 128])
        bsb, bsbb = sbt(c, ph, "bsbs", [128, 4, 128], F32)
        Msel, Mselb = sbt(c, ph, "Msel", [128, 2, 5, 512]); Mwin, Mwinb = sbt(c, ph, "Mwin", [128, 2, 8, 512]); NCm, NCmb = sbt(c, ph, "NCm", [40, 2, 512])
        Jw, Jwb = sbt(c, ph, "Jw", [40, 272]); I4, I4b = sbt(c, ph, "I4", [128, 512])
        cand, candb = sbt(c, ph, "cand", [128, 256], F32); forc, forcb = sbt(c, ph, "forc", [128, 256], F32)

        for (t_, b_, src) in [(Wu, Wub, w_u), (Wv, Wvb, w_v), (Wq, Wqb, w_q), (Wng, Wngb, w_ng)]:
            load(c, "gpsimd", t_[:], b_, src.rearrange("(kc p) n -> p kc n", p=128))
        load(c, "sync", brow[:, 0, :], browb, b_u); load(c, "sync", brow[:, 1, :], browb, b_v)
        load(c, "sync", brow[:, 2, :], browb, b_q); load(c, "sync", brow[:, 3, 0:24], browb, b_ng)
        load(c, "sync", lngs[:], lngb, lng); load(c, "sync", lnbs[:], lnbb, lnb)
        load(c, "sync", bsb[:], bsbb, bsb_d)
        bqc, bqcb = sbt(c, ph, "bqc", [128, 4], F32)
        load(c, "sync", bqc[:], bqcb, b_qc)
        load(c, "gpsimd", Jw[:], Jwb, jw_d); load(c, "gpsimd", I4[:], I4b, i4_d)
        load(c, "sync", cand[:], candb, cand_d); load(c, "sync", forc[:], forcb, forc_d)
        ph2 = contextlib.ExitStack()
        stg = [sbt(c, ph2, "stg%d" % i, [128, 512], F32) for i in range(2)]
        fars = [sbt(c, ph2, "far%d" % i, [128, 512], F32) for i in range(2)]
        ngs = [sbt(c, ph2, "ngs%d" % i, [128, 512], F32) for i in range(2)]
        wsf, wsfb = sbt(c, ph2, "wsf", [128, 8, 128], F32); tri, trib = sbt(c, ph2, "tri", [128, 128], F32)
        load(c, "sync", wsf[:], wsfb, wsT); load(c, "sync", tri[:], trib, triu)
        for g8 in range(8):
            S.op("vector", lambda e, g8=g8: e.tensor_tensor(out=WsT[:, g8, :], in0=wsf[:, g8, :], in1=tri[:], op=ALU.mult), reads=(wsfb, trib), writes=(WsTb,))
        for g in range(2):
            load(c, "sync", fars[g][0][:], fars[g][1], far_d[g])
        k = 0
        jobs = []
        for g in range(2):
            for e_ in range(5):
                jobs.append((bsel_d[g, e_], nsel_d[e_], g, Msel[:, g, e_, :], Mselb, 128))
            for e_ in range(8):
                jobs.append((bwin_d[g, e_], nwin_d[e_], g, Mwin[:, g, e_, :], Mwinb, 128))
            jobs.append((bcmp_d[g], ncmp_d, g, NCm[:, g, :], NCmb, 40))
        for (bsrc, nsrc, g, dst, dstb, npart) in jobs:
            st_, stb_ = stg[k % 2]; ng_, ngb_ = ngs[k % 2]; k += 1
            load(c, "sync", st_[0:npart, :], stb_, bsrc)
            load(c, "sync", ng_[0:npart, :], ngb_, nsrc)
            S.op("vector", lambda e, st_=st_, g=g, npart=npart: e.tensor_tensor(out=st_[0:npart, :], in0=st_[0:npart, :], in1=fars[g][0][0:npart, :], op=ALU.subtract),
                 reads=(stb_, fars[g][1]), writes=(stb_,))
            S.op("vector", lambda e, st_=st_, ng_=ng_, dst=dst, npart=npart: e.tensor_tensor(out=dst, in0=st_[0:npart, :], in1=ng_[0:npart, :], op=ALU.add),
                 reads=(stb_, ngb_), writes=(dstb,))

        S.flush()
        ph2.close()
        if DBG.get("stop") == 2:
            ph.close(); c.top.close()
            return nc
        hTo = [sbt(c, ph, "hTo%d" % i, [128, 8, 128]) for i in range(2)]
        QZ, QZb0 = sbt(c, ph, "QZ", [128, 2, 2, 4, 128])
        QZbs = [QZb0, Buf("QZ1")]
        S.op("vector", lambda e: e.memset(QZ[:], 0.0), writes=(QZbs[0], QZbs[1]))
        xs, xsb = sbt(c, ph, "xs", [128, 512], F32); tt, ttb = sbt(c, ph, "tt", [128, 512], F32)
        gus = [sbt(c, ph, "gu%d" % i, [128, 512], F32) for i in range(2)]; gv, gvb = sbt(c, ph, "gv", [128, 512], F32)
        vlns = [sbt(c, ph, "vln%d" % i, [128, 512]) for i in range(2)]
        aTs = [sbt(c, ph, "aTs%d" % i, [128, 512]) for i in range(2)]
        oTs = [sbt(c, ph, "oTs%d" % i, [128, 512]) for i in range(2)]
        gates = [sbt(c, ph, "gate%d" % i, [128, 24], F32) for i in range(2)]
        osb, osbb = sbt(c, ph, "osb", [128, 512], F32)
        PT = [sbt(c, ph, "PT%d" % i, [128, 512]) for i in range(3)]
        negr = [sbt(c, ph, "negx%d" % i, [128, 2048]) for i in range(2)]
        imp, impb = sbt(c, ph, "imp", [128, 128], F32); imp2, imp2b = sbt(c, ph, "imp2", [128, 128], F32)
        m8a, m8ab = sbt(c, ph, "m8a", [128, 8], F32); m8b, m8bb = sbt(c, ph, "m8b", [128, 8], F32)
        selm, selmb = sbt(c, ph, "selm", [128, 128], F32); nmk, nmkb = sbt(c, ph, "nmk", [128, 128])
        rden, rdenb = sbt(c, ph, "rden", [128, 4], F32); wgt, wgtb = sbt(c, ph, "wgt", [128, 4], F32)
        ptc = [0]
        sc = [0]
        nxc = [0]

        def next_S():
            b = sc[0] % 3
            sc[0] += 1
            return b

        def exp_pv(bS, nk, Obanks, ocol, ow, vfn, vbuf, last, mT=None, mTb=None):
            pi = ptc[0] % 3
            ptc[0] += 1
            P_, Pb_ = PT[pi]
            S.op("scalar", lambda e, bS=bS, nk=nk, P_=P_: e.activation(out=P_[0:nk, :], in_=ps[bS][0:nk, :], func=AF.Exp), reads=(psb[bS],), writes=(Pb_,))
            if mT is not None:
                S.op("vector", lambda e, P_=P_, mT=mT: e.tensor_tensor(out=P_[:, :].rearrange("p (h q) -> p h q", h=4), in0=P_[:, :].rearrange("p (h q) -> p h q", h=4),
                                                                     in1=mT.unsqueeze(1).to_broadcast([128, 4, 128]), op=ALU.mult),
                     reads=(Pb_, mTb), writes=(Pb_,))
            vap = vfn(nk)

            def do_pv():
                for hp in range(4):
                    ob = Obanks[hp]
                    mm(c, ps[ob][:, ocol[hp]:ocol[hp] + ow], psb[ob], P_[0:nk, hp * 128:(hp + 1) * 128], vap, False, last, (Pb_, vbuf))
            pend.append(do_pv)
            while len(pend) > 2:
                pend.pop(0)()

        pend = []

        def flush_pv():
            while pend:
                pend.pop(0)()

        def zero_start(ob, width):
            mm(c, ps[ob][:, 0:width], psb[ob], zer[0:1, 0:128], zer[0:1, 0:width], True, False, (zerb,))

        def finalize(Obanks, ocol, g, br, first):
            for hp in range(4):
                ob = Obanks[hp]
                S.op("vector", lambda e, ob=ob, hp=hp, cc_=ocol[hp]: e.tensor_scalar_max(out=rden[:, hp:hp + 1], in0=ps[ob][:, cc_ + 64:cc_ + 65], scalar1=1e-20),
                     reads=(psb[ob],), writes=(rdenb,))
            S.op("vector", lambda e: e.reciprocal(out=rden[:], in_=rden[:]), reads=(rdenb,), writes=(rdenb,))
            gsl = gate[:, g * 12 + br:g * 12 + br + 10:3]
            S.op("vector", lambda e, gsl=gsl: e.tensor_tensor(out=wgt[:], in0=rden[:], in1=gsl, op=ALU.mult), reads=(rdenb, gateb), writes=(wgtb,))
            for hp in range(4):
                ob = Obanks[hp]
                od = osb[:, (g * 4 + hp) * 64:(g * 4 + hp) * 64 + 64]
                src = ps[ob][:, ocol[hp]:ocol[hp] + 64]
                if first:
                    S.op("vector", lambda e, od=od, src=src, hp=hp: e.tensor_scalar(out=od, in0=src, scalar1=wgt[:, hp:hp + 1], scalar2=None, op0=ALU.mult),
                         reads=(psb[ob], wgtb), writes=(osbb,))
                else:
                    S.op("vector", lambda e, od=od, src=src, hp=hp: e.scalar_tensor_tensor(out=od, in0=src, scalar=wgt[:, hp:hp + 1], in1=od, op0=ALU.mult, op1=ALU.add),
                         reads=(psb[ob], wgtb, osbb), writes=(osbb,))

        lnv_em = {}

        def proj_A(s):
            par = s % 2
            ht, htb = hTo[par]
            load(c, "gpsimd", ht[:], htb, hT_own[:, s * 128:(s + 1) * 128].rearrange("(kc p) t -> p kc t", p=128))
            for m in range(4):
                for kc in range(8):
                    mm(c, ps[7][:, m * 128:(m + 1) * 128], psb[7], Wu[:, kc, m * 128:(m + 1) * 128], ht[:, kc, :], kc == 0, False, (Wub, htb))
                mm(c, ps[7][:, m * 128:(m + 1) * 128], psb[7], brow[0:1, 0, m * 128:(m + 1) * 128], ones[0:1, 0:128], False, True, (browb, onesb))
            for kc in range(8):
                mm(c, ps[0][:, :], psb[0], ht[:, kc, :], Wv[:, kc, :], kc == 0, False, (Wvb, htb))
            mm(c, ps[0][:, :], psb[0], ones[0:1, 0:128], brow[0:1, 1, :], False, True, (browb, onesb))
            for m in range(4):
                for kc in range(8):
                    mm(c, ps[1][:, m * 128:(m + 1) * 128], psb[1], Wq[:, kc, m * 128:(m + 1) * 128], ht[:, kc, :], kc == 0, False, (Wqb, htb))
                mm(c, ps[1][:, m * 128:(m + 1) * 128], psb[1], brow[0:1, 2, m * 128:(m + 1) * 128], ones[0:1, 0:128], False, True, (browb, onesb))
            for kc in range(8):
                mm(c, ps[2][:, 0:24], psb[2], ht[:, kc, :], Wng[:, kc, :], kc == 0, False, (Wngb, htb))
            mm(c, ps[2][:, 0:24], psb[2], ones[0:1, 0:128], brow[0:1, 3, 0:24], False, True, (browb, onesb))
            gu_, gub_ = gus[par]
            vl_, vlb_ = vlns[par]
            S.op("scalar", lambda e: e.activation(out=xs[:], in_=ps[7][:, :], func=AF.Identity), reads=(psb[7],), writes=(xsb,))
            gelu_chain(c, xs[:], xsb, tt[:], ttb, gu_[:], gub_)
            S.op("scalar", lambda e: e.activation(out=xs[:], in_=ps[0][:, :], func=AF.Identity), reads=(psb[0],), writes=(xsb,))
            gelu_chain(c, xs[:], xsb, tt[:], ttb, gv[:], gvb)
            if par not in lnv_em:
                lnv_em[par] = layer_norm_rows(c, ph, gv, gvb, 1, lngs, lngb, lnbs, lnbb, vl_[:], vlb_, "v%d" % par)
            lnv_em[par]()
            for g in range(2):
                S.op("vector", lambda e, g=g, par=par: e.tensor_scalar(out=QZ[64 * g:64 * g + 64, par, g, :, :].rearrange("p a b -> p (a b)"), in0=ps[1][64 * g:64 * g + 64, :],
                                                                     scalar1=0.125, scalar2=None, op0=ALU.mult),
                     reads=(psb[1],), writes=(QZbs[par],))
            gt_, gtb_ = gates[par]
            S.op("scalar", lambda e, gt_=gt_: e.activation(out=gt_[:], in_=ps[2][:, 0:24], func=AF.Sigmoid), reads=(psb[2],), writes=(gtb_,))

        def proj_B(s):
            par = s % 2
            gu_, gub_ = gus[par]
            vl_, vlb_ = vlns[par]
            for g8 in range(8):
                m, half = g8 // 2, g8 % 2
                mm(c, ps[7][64 * half:64 * half + 64, m * 128:(m + 1) * 128], psb[7], vl_[:, g8 * 64:(g8 + 1) * 64], WsT[:, g8, :], True, True, (vlb_, WsTb))
            S.op("vector", lambda e: e.tensor_tensor(out=xs[:], in0=ps[7][:, :], in1=bsb[:].rearrange("p a b -> p (a b)"), op=ALU.add),
                 reads=(psb[7], bsbb), writes=(xsb,))
            at_, atb_ = aTs[par]
            S.op("vector", lambda e, at_=at_, gu_=gu_: e.tensor_tensor(out=at_[:], in0=xs[:], in1=gu_[:], op=ALU.mult),
                 reads=(xsb, gub_), writes=(atb_,))
            S.dma("sync", lambda e, at_=at_, s=s: e.dma_start(out=aT_d[s], in_=at_[:]), reads=(atb_,), writes=(aTdb,))

        nb2 = DBG.get("nb2", NB)
        proj_A(0)
        for s in range(nb2):
            proj_B(s)
            if s + 1 < nb2:
                proj_A(s + 1)
            gate, gateb = gates[s % 2]
            QZb = QZbs[s % 2]
            par_s = s % 2

            for g in range(DBG.get("ng", 2)):
                Qg = QZ[:, par_s, g, :, :].rearrange("p a b -> p (a b)")
                ncmp = 32 * s + 31
                nch = (ncmp + 127) // 128
                Ob = [5, 5, 6, 6]; oc = [0, 193, 0, 193]
                zero_start(5, 386); zero_start(6, 386)
                for cc in range(nch):
                    nk = min(128, ncmp - 128 * cc)
                    bS = next_S()
                    shift = 137 + 128 * cc - 32 * s
                    near = (128 - shift + 39 >= 0) and (128 - shift < nk) and shift >= 0
                    mm(c, ps[bS][0:nk, :], psb[bS], KcT[:, 128 * cc:128 * cc + nk], Qg, True, not near, (KcTb, QZb))
                    if near:
                        mm(c, ps[bS][0:nk, :], psb[bS], Jw[0:40, shift:shift + nk], NCm[0:40, g, :], False, True, (Jwb, NCmb))
                    exp_pv(bS, nk, Ob, oc, 193, lambda nk, cc=cc, g=g: Vca[0:nk, cc, g, :], Vcab, cc == nch - 1)
                flush_pv()
                finalize(Ob, oc, g, 0, True)
                use_mask = s >= 2
                if use_mask:
                    for hp in range(4):
                        src = ps[Ob[hp]][:, oc[hp] + 65:oc[hp] + 193]
                        if hp == 0:
                            S.op("vector", lambda e, src=src: e.tensor_scalar(out=imp[:], in0=src, scalar1=rden[:, 0:1], scalar2=None, op0=ALU.mult),
                                 reads=(psb[Ob[hp]], rdenb), writes=(impb,))
                        else:
                            S.op("vector", lambda e, src=src, hp=hp: e.scalar_tensor_tensor(out=imp[:], in0=src, scalar=rden[:, hp:hp + 1], in1=imp[:], op0=ALU.mult, op1=ALU.add),
                                 reads=(psb[Ob[hp]], rdenb, impb), writes=(impb,))
                    csl = cand[:, 128 - 8 * s:256 - 8 * s]
                    fsl = forc[:, 128 - 8 * s:256 - 8 * s]
                    S.op("vector", lambda e, csl=csl: e.tensor_tensor(out=imp[:], in0=imp[:], in1=csl, op=ALU.mult), reads=(impb, candb), writes=(impb,))
                    S.op("vector", lambda e: e.max(out=m8a[:], in_=imp[:]), reads=(impb,), writes=(m8ab,))
                    S.op("vector", lambda e: e.match_replace(out=imp2[:], in_to_replace=m8a[:], in_values=imp[:], imm_value=-1.0), reads=(impb, m8ab), writes=(imp2b,))
                    S.op("vector", lambda e: e.max(out=m8b[:], in_=imp2[:]), reads=(imp2b,), writes=(m8bb,))
                    S.op("vector", lambda e: e.tensor_scalar(out=selm[:], in0=imp[:], scalar1=m8b[:, 4:5], scalar2=None, op0=ALU.is_ge), reads=(impb, m8bb), writes=(selmb,))
                    S.op("vector", lambda e, fsl=fsl: e.tensor_tensor(out=selm[:], in0=selm[:], in1=fsl, op=ALU.max), reads=(selmb, forcb), writes=(selmb,))
                    S.op("vector", lambda e: e.tensor_copy(out=nmk[:], in_=selm[:]), reads=(selmb,), writes=(nmkb,))
                    S.op("vector", lambda e: e.memset(nmk[:, 0:1], 1.0), reads=(), writes=(nmkb,))
                Ob = [4, 4, 4, 4]; oc = [0, 65, 130, 195]
                zero_start(4, 260)
                wl = [e_ for e_ in range(-4, 4) if 4 * s + e_ >= 0]
                for e_ in wl:
                    ck = 4 * s + e_
                    bS = next_S()
                    mm(c, ps[bS][:, :], psb[bS], KwT[:, 128 * ck:128 * ck + 128], Qg, True, False, (KwTb, QZb))
                    mm(c, ps[bS][:, :], psb[bS], I4[:, 0:128], Mwin[:, g, e_ + 4, :], False, True, (I4b, Mwinb))
                    exp_pv(bS, 128, Ob, oc, 65, lambda nk, ck=ck, g=g: Vw[:, ck, g, :], Vwb, e_ == wl[-1])
                flush_pv()
                finalize(Ob, oc, g, 2, False)
                Ob = [3, 3, 3, 3]; oc = [0, 65, 130, 195]
                zero_start(3, 260)
                nsc = 4 * s + 4
                for ck in range(nsc):
                    bS = next_S()
                    e_ = ck - 4 * s
                    extra = (1 if e_ >= -1 else 0)
                    mm(c, ps[bS][:, :], psb[bS], KsT[:, 128 * ck:128 * ck + 128], Qg, True, extra == 0, (KsTb, QZb))
                    mT_, mTb_ = None, None
                    if use_mask:
                        if ck % 16 == 0:
                            nx, nxb = negr[nxc[0] % 2]
                            nxc[0] += 1
                            nb_ = min(32, 2 * nsc - 2 * ck)
                            S.op("gpsimd", lambda e, nx=nx, nb_=nb_, ck=ck: e.tensor_copy(out=nx[:, 0:nb_ * 64].rearrange("p (a b) -> p a b", b=64),
                                                                                        in_=nmk[:, 2 * ck:2 * ck + nb_].unsqueeze(2).to_broadcast([128, nb_, 64])),
                                 reads=(nmkb,), writes=(nxb,))
                        cl = ck % 16
                        tb_ = 5 + (ck % 2)
                        mT_ = ps[tb_][:, 0:64].bitcast(BF16)
                        mTb_ = psb[tb_]
                        S.op("tensor", lambda e, mT_=mT_, nx=nx, cl=cl: e.transpose(out=mT_, in_=nx[:, 128 * cl:128 * cl + 128], identity=I4[:, 0:128]),
                             reads=(nxb, I4b), writes=(mTb_,))
                    if e_ >= -1:
                        mm(c, ps[bS][:, :], psb[bS], I4[:, 0:128], Msel[:, g, e_ + 1, :], False, True, (I4b, Mselb))
                    exp_pv(bS, 128, Ob, oc, 65, lambda nk, ck=ck, g=g: Vs[:, ck, g, :], Vsb, ck == nsc - 1, mT_, mTb_)
                flush_pv()
                finalize(Ob, oc, g, 1, False)
            for m in range(4):
                S.op("tensor", lambda e, m=m: e.transpose(out=ps[7][:, m * 128:(m + 1) * 128], in_=osb[:, m * 128:(m + 1) * 128], identity=identf[:]),
                     reads=(osbb, identb), writes=(psb[7],))
            ot_, otb_ = oTs[s % 2]
            S.op("scalar", lambda e, ot_=ot_: e.activation(out=ot_[:], in_=ps[7][:, :], func=AF.Identity),
                 reads=(psb[7],), writes=(otb_,))
            S.dma("sync", lambda e, ot_=ot_, s=s: e.dma_start(out=oT_d[s], in_=ot_[:]), reads=(otb_,), writes=(oTdb,))
        S.flush()
    if DBG.get("stop") == 3:
        c.top.close()
        return nc

    with contextlib.ExitStack() as ph:
        Wmg, Wmgb = sbt(c, ph, "Wmg", [128, 8, 2048]); Wpa, Wpab = sbt(c, ph, "Wpa", [128, 4, D]); Wpb, Wpbb = sbt(c, ph, "Wpb", [128, 4, D])
        Wo, Wob = sbt(c, ph, "Wo", [128, 8, D]); bmg, bmgb = sbt(c, ph, "bmg", [1, 2048], F32)
        g1, g1b = sbt(c, ph, "g1", [128, D], F32); b1_, b1b_ = sbt(c, ph, "b1_", [128, D], F32)
        hTo = [sbt(c, ph, "hTp%d" % i, [128, 8, 128]) for i in range(2)]
        hb = [sbt(c, ph, "hb%d" % i, [128, D], F32) for i in range(2)]
        aTl = [sbt(c, ph, "aTl%d" % i, [128, 4, 128]) for i in range(2)]
        oTl = [sbt(c, ph, "oTl%d" % i, [128, 4, 128]) for i in range(2)]
        sga, sgab = sbt(c, ph, "sga", [128, 512], F32); sgb, sgbb = sbt(c, ph, "sgb", [128, 512], F32)
        y1, y1b = sbt(c, ph, "y1", [128, 512], F32)
        yT, yTb = sbt(c, ph, "yT", [128, 8, 128])
        rr, rrb = sbt(c, ph, "rr", [128, D], F32)
        ob_ = [sbt(c, ph, "ob%d" % i, [128, D], F32) for i in range(2)]
        def load_blk3(s):
            ht, htb = hTo[s % 2]
            hbt, hbb = hb[s % 2]
            load(c, "gpsimd", ht[:], htb, hT_own[:, s * 128:(s + 1) * 128].rearrange("(kc p) t -> p kc t", p=128))
            load(c, "sync", hbt[:], hbb, h_own[s * 128:(s + 1) * 128, :])
            aT, aTb = aTl[s % 2]; oT, oTb = oTl[s % 2]
            S.dma("sync", lambda e, aT=aT, s=s: e.dma_start(out=aT[:].rearrange("p a b -> p (a b)"), in_=aT_d[s]), reads=(aTdb,), writes=(aTb,))
            S.dma("sync", lambda e, oT=oT, s=s: e.dma_start(out=oT[:].rearrange("p a b -> p (a b)"), in_=oT_d[s]), reads=(oTdb,), writes=(oTb,))
        bmgc, bmgcb = sbt(c, ph, "bmgc", [128, 16], F32)
        load(c, "sync", bmgc[:], bmgcb, b_mgc)
        load_blk3(0)
        Wmgq = [Buf("Wmgq%d" % i) for i in range(4)]
        wmg_v = w_mg.rearrange("(kc p) n -> p kc n", p=128)
        for q4 in (0, 2):
            load(c, "gpsimd", Wmg[:, :, q4 * 512:(q4 + 1) * 512], Wmgq[q4], wmg_v[:, :, q4 * 512:(q4 + 1) * 512])
        load(c, "gpsimd", Wpa[:], Wpab, w_pa.rearrange("(kc p) n -> p kc n", p=128))
        load(c, "gpsimd", Wpb[:], Wpbb, w_pb.rearrange("(kc p) n -> p kc n", p=128))
        for q4 in (1, 3):
            load(c, "gpsimd", Wmg[:, :, q4 * 512:(q4 + 1) * 512], Wmgq[q4], wmg_v[:, :, q4 * 512:(q4 + 1) * 512])
        load(c, "gpsimd", Wo[:], Wob, w_o.rearrange("(kc p) n -> p kc n", p=128))
        load(c, "sync", g1[:], g1b, ln1g); load(c, "sync", b1_[:], b1b_, ln1b)
        ln1 = None
        for s in range(NB):
            ht, htb = hTo[s % 2]
            hbt, hbb = hb[s % 2]
            aT, aTb = aTl[s % 2]; oT, oTb = oTl[s % 2]
            if s + 1 < NB:
                load_blk3(s + 1)
            for half in range(2):
                for (bk, coff, sg_, sgb_) in [(0, half * 512, sga, sgab), (1, 1024 + half * 512, sgb, sgbb)]:
                    for m in range(4):
                        cs = coff + m * 128
                        for kc in range(8):
                            mm(c, ps[bk][:, m * 128:(m + 1) * 128], psb[bk], Wmg[:, kc, cs:cs + 128], ht[:, kc, :], kc == 0, kc == 7, (Wmgq[cs // 512], htb))
                    for m in range(4):
                        ti = (coff + m * 128) // 128
                        S.op("scalar", lambda e, bk=bk, sg_=sg_, m=m, ti=ti: e.activation(out=sg_[:, m * 128:(m + 1) * 128], in_=ps[bk][:, m * 128:(m + 1) * 128], func=AF.Sigmoid,
                                                                                    bias=bmgc[:, ti:ti + 1], scale=1.0),
                             reads=(psb[bk], bmgcb), writes=(sgb_,))
                for (bk, W_, Wb_, xT, xTb) in [(2, Wpa, Wpab, aT, aTb), (3, Wpb, Wpbb, oT, oTb)]:
                    for m in range(4):
                        cs = half * 512 + m * 128
                        for k4 in range(4):
                            mm(c, ps[bk][:, m * 128:(m + 1) * 128], psb[bk], W_[:, k4, cs:cs + 128], xT[:, k4, :], k4 == 0, k4 == 3, (Wb_, xTb))
                S.op("vector", lambda e: e.tensor_tensor(out=y1[:], in0=sga[:], in1=ps[2][:, :], op=ALU.mult), reads=(sgab, psb[2]), writes=(y1b,))
                S.op("vector", lambda e: e.tensor_tensor(out=sgb[:], in0=sgb[:], in1=ps[3][:, :], op=ALU.mult), reads=(sgbb, psb[3]), writes=(sgbb,))
                S.op("vector", lambda e, half=half: e.tensor_tensor(out=yT[:, half * 4:half * 4 + 4, :].rearrange("p a b -> p (a b)"), in0=y1[:], in1=sgb[:], op=ALU.add),
                     reads=(y1b, sgbb), writes=(yTb,))
            for half in range(2):
                bk = 4 + half
                for m in range(8):
                    mm(c, ps[bk][:, :], psb[bk], yT[:, m, :], Wo[:, m, half * 512:(half + 1) * 512], m == 0, m == 7, (yTb, Wob))
                S.op("vector", lambda e, half=half, bk=bk, hbt=hbt: e.scalar_tensor_tensor(out=rr[:, half * 512:(half + 1) * 512], in0=hbt[:, half * 512:(half + 1) * 512],
                                                                                         scalar=ALPHA, in1=ps[bk][:, :], op0=ALU.mult, op1=ALU.add),
                     reads=(hbb, psb[bk]), writes=(rrb,))
            ot, otb = ob_[s % 2]
            if ln1 is None:
                ln1 = {}
            key = s % 2
            if key not in ln1:
                ln1[key] = layer_norm_rows(c, ph, rr, rrb, 2, g1, g1b, b1_, b1b_, ot[:], otb, "1_%d" % key)
            ln1[key]()
            c.S.dma("sync", lambda e, ot=ot, s=s: e.dma_start(out=h1_out[s * 128:(s + 1) * 128, :], in_=ot[:]), reads=(otb,), writes=(outb,))
        S.wait_all_dma("sync")
        S.flush()
    c.top.close()
    return nc


def build_C():
    c = new_ctx()
    nc, S, ps, psb = c.nc, c.S, c.ps, c.psb
    top = c.top
    I = lambda n, s: dram_in(c, n, s)
    h1T = I("h1T", [D, NB, 130]); h1 = I("h1", [2048, D])
    w_up = I("w_up", [D, 2 * DFF]); w_dn = I("w_dn", [DFF, D])
    cw = I("cw", [128, 44, 4]); ln2g = I("ln2g", [128, D]); ln2b = I("ln2b", [128, D])
    h2_out = nc.dram_tensor("h2_out", [2048, D], F32, kind="ExternalOutput").ap()
    outb = Buf("h2_out")
    with contextlib.ExitStack() as ph:
        Wup, Wupb = sbt(c, ph, "Wup", [128, 8, 2 * DFF]); Wdn, Wdnb = sbt(c, ph, "Wdn", [128, 22, D])
        cws, cwb = sbt(c, ph, "cws", [128, 44, 4], F32)
        g2, g2b = sbt(c, ph, "g2", [128, D], F32); b2, b2b = sbt(c, ph, "b2", [128, D], F32)
        GB = 2
        hT = [sbt(c, ph, "hT%d" % i, [128, 8, GB * 130]) for i in range(2)]
        hb = [sbt(c, ph, "hb%d" % i, [128, GB, D], F32) for i in range(2)]
        cg = [sbt(c, ph, "cg%d" % i, [128, GB, 128], F32) for i in range(2)]
        cv = [sbt(c, ph, "cv%d" % i, [128, GB, 128], F32) for i in range(2)]
        sT, sTb = sbt(c, ph, "sT", [128, 22, GB, 128])
        rr, rrb = sbt(c, ph, "rr", [128, D], F32)
        ob_ = [sbt(c, ph, "ob%d" % i, [128, D], F32) for i in range(2)]
        def load_pair(sp):
            ht, htb = hT[sp % 2]
            hbt, hbb = hb[sp % 2]
            for bi in range(GB):
                s = sp * GB + bi
                load(c, "gpsimd", ht[:, :, bi * 130:(bi + 1) * 130], htb, h1T[:, s, :].rearrange("(kc p) t -> p kc t", p=128))
                load(c, "sync", hbt[:, bi, :], hbb, h1[s * 128:(s + 1) * 128, :])
        load(c, "sync", cws[:], cwb, cw)
        load_pair(0)
        Wupq = [Buf("Wupq%d" % i) for i in range(4)]
        for q4 in (0, 2, 1, 3):
            load(c, "gpsimd", Wup[:, :, q4 * 1408:(q4 + 1) * 1408], Wupq[q4], w_up[:, q4 * 1408:(q4 + 1) * 1408].rearrange("(kc p) n -> p kc n", p=128))
        load(c, "gpsimd", Wdn[:], Wdnb, w_dn.rearrange("(kc p) n -> p kc n", p=128))
        load(c, "sync", g2[:], g2b, ln2g); load(c, "sync", b2[:], b2b, ln2b)
        ln2 = {}

        def conv(bk, ft, dst, dstb):
            pv = ps[bk][:, 0:GB * 130].rearrange("p (b t) -> p b t", b=GB)
            S.op("vector", lambda e: e.tensor_scalar(out=dst[:], in0=pv[:, :, 0:128], scalar1=cws[:, ft, 0:1], scalar2=cws[:, ft, 3:4], op0=ALU.mult, op1=ALU.add),
                 reads=(psb[bk], cwb), writes=(dstb,))
            S.op("vector", lambda e: e.scalar_tensor_tensor(out=dst[:], in0=pv[:, :, 1:129], scalar=cws[:, ft, 1:2], in1=dst[:], op0=ALU.mult, op1=ALU.add),
                 reads=(psb[bk], cwb, dstb), writes=(dstb,))
            S.op("vector", lambda e: e.scalar_tensor_tensor(out=dst[:], in0=pv[:, :, 2:130], scalar=cws[:, ft, 2:3], in1=dst[:], op0=ALU.mult, op1=ALU.add),
                 reads=(psb[bk], cwb, dstb), writes=(dstb,))

        for sp in range(NB // GB):
            ht, htb = hT[sp % 2]
            hbt, hbb = hb[sp % 2]
            for f in range(22):
                bg, bv = (f % 2) * 2, (f % 2) * 2 + 1
                cgt, cgb = cg[f % 2]
                cvt, cvb = cv[f % 2]
                for (bk, ft) in [(bg, f), (bv, 22 + f)]:
                    for kc in range(8):
                        mm(c, ps[bk][:, 0:GB * 130], psb[bk], Wup[:, kc, ft * 128:(ft + 1) * 128], ht[:, kc, :], kc == 0, kc == 7, (Wupq[ft // 11], htb))
                if f == 0 and sp + 1 < NB // GB:
                    load_pair(sp + 1)
                conv(bg, f, cgt, cgb)
                conv(bv, 22 + f, cvt, cvb)
                S.op("scalar", lambda e, cgt=cgt: e.activation(out=cgt[:], in_=cgt[:], func=AF.Silu), reads=(cgb,), writes=(cgb,))
                S.op("vector", lambda e, f=f, cgt=cgt, cvt=cvt: e.tensor_tensor(out=sT[:, f, :, :], in0=cgt[:], in1=cvt[:], op=ALU.mult), reads=(cgb, cvb), writes=(sTb,))
            for bi in range(GB):
                s = sp * GB + bi
                for half in range(2):
                    bk = 4 + half
                    for f in range(22):
                        mm(c, ps[bk][:, :], psb[bk], sT[:, f, bi, :], Wdn[:, f, half * 512:(half + 1) * 512], f == 0, f == 21, (sTb, Wdnb))
                    S.op("vector", lambda e, half=half, bk=bk, hbt=hbt, bi=bi: e.scalar_tensor_tensor(out=rr[:, half * 512:(half + 1) * 512], in0=hbt[:, bi, half * 512:(half + 1) * 512],
                                                                                                 scalar=ALPHA, in1=ps[bk][:, :], op0=ALU.mult, op1=ALU.add),
                         reads=(hbb, psb[bk]), writes=(rrb,))
                ot, otb = ob_[s % 2]
                key = s % 2
                if key not in ln2:
                    ln2[key] = layer_norm_rows(c, ph, rr, rrb, 2, g2, g2b, b2, b2b, ot[:], otb, "2_%d" % key)
                ln2[key]()
                c.S.dma("sync", lambda e, ot=ot, s=s: e.dma_start(out=h2_out[s * 128:(s + 1) * 128, :], in_=ot[:]), reads=(otb,), writes=(outb,))
        S.wait_all_dma("sync")
        S.flush()
    c.top.close()
    return nc


def t5_bucket_np(dist):
    n = np.maximum(dist, 0)
    max_exact = 16
    lr = np.log(np.maximum(n, 1).astype(np.float32) / np.float32(max_exact)) / np.float32(math.log(128 / 16))
    large = np.minimum(max_exact + (lr.astype(np.float32) * np.float32(16)).astype(np.int32), 31)
    return np.where(n < max_exact, n, large).astype(np.int64)


def own_index(j):
    return np.concatenate([128 * (4 * s + j) + np.arange(128) for s in range(NB)])


_CONST = {}


def core_consts(j):
    if j in _CONST:
        return _CONST[j]
    k = np.arange(128)[:, None]
    q = np.arange(128)[None, :]
    cs = {}
    dsel = [128 * (j - e) + q - k for e in range(-1, 4)]
    dwin = [128 * (j - e) + q - k for e in range(-4, 4)]
    y = np.arange(40)[:, None]
    dcmp = 128 * j + q - 16 * y + 113
    cs["isel"] = [t5_bucket_np(d) for d in dsel]
    cs["iwin"] = [t5_bucket_np(d) for d in dwin]
    cs["icmp"] = t5_bucket_np(dcmp)
    f32 = np.float32
    cs["nsel"] = np.stack([np.tile(np.where(d >= 0, 0.0, NEG).astype(f32), (1, 4)) for d in dsel])
    cs["nwin"] = np.stack([np.tile(np.where((d >= 0) & (d < 512), 0.0, NEG).astype(f32), (1, 4)) for d in dwin])
    cs["ncmp"] = np.tile(np.where(dcmp >= 0, 0.0, NEG).astype(f32), (1, 4))
    x = np.arange(256)[None, :] - 2 * j
    qq = np.arange(128)[:, None]
    hi = (qq >= 64).astype(np.int64)
    cs["cand"] = (x <= 126 + hi).astype(f32)
    cs["forc"] = ((x == 127 + hi) | (x == 128 + hi)).astype(f32)
    _CONST[j] = cs
    return cs


def shared_consts():
    if "sh" in _CONST:
        return _CONST["sh"]
    f32 = np.float32
    sh = {}
    jw = np.zeros((40, 272), f32)
    jw[np.arange(40), np.arange(40) + 128] = 1.0
    sh["jw"] = jw
    sh["i4"] = np.tile(np.eye(128, dtype=f32), (1, 4))
    sh["triu"] = np.triu(np.ones((128, 128), f32))
    n_cmp = SEQ // 16 - 1
    cstart = np.arange(512)[:, None] * 16
    sstart = np.arange(128)[None, :] * 64
    ov = np.minimum(cstart + 32, sstart + 64) - np.maximum(cstart, sstart)
    ov = (np.maximum(ov, 0) / 32).astype(f32)
    ov[n_cmp:, :] = 0.0
    ov[:, 0] = 0.0
    sh["ovl"] = np.ascontiguousarray(ov.reshape(4, 128, 128).transpose(1, 0, 2))
    _CONST["sh"] = sh
    return sh


def rep128(v):
    return np.ascontiguousarray(np.broadcast_to(np.asarray(v, np.float32)[None, :], (128, v.shape[0])))


def layer_inputs_B(l, P):
    f32 = np.float32
    w_in = P["w_in"][l]; b_in = P["b_in"][l]
    o_u, o_v, o_q, o_kc, o_vc, o_ks, o_vs, o_kw, o_vw, o_ng, o_mg = 0, 512, 1024, 1536, 1664, 1792, 1920, 2048, 2176, 2304, 2328
    fcols = np.concatenate([np.arange(o_kc, o_kc + 128), np.arange(o_vc, o_vc + 128), np.arange(o_ks, o_ks + 128), np.arange(o_kw, o_kw + 128)])
    tcols = np.concatenate([np.arange(o_vs, o_vs + 128), np.arange(o_vw, o_vw + 128)])
    qcols = np.concatenate([np.concatenate([o_q + hp * 64 + np.arange(64), o_q + (4 + hp) * 64 + np.arange(64)]) for hp in range(4)])
    C = np.ascontiguousarray
    sh = shared_consts()
    d = {}
    d["w_kvf"] = C(w_in[:, fcols]); d["w_kvt"] = C(w_in[:, tcols]); d["b_kvf"] = C(b_in[fcols][None]); d["b_kvt"] = C(b_in[tcols][None]); d["b_kvfc"] = C(b_in[fcols].reshape(4, 128).T); d["b_kvr"] = rep128(b_in[tcols])
    d["w_u"] = C(w_in[:, o_u:o_u + 512]); d["w_v"] = C(w_in[:, o_v:o_v + 512]); d["w_q"] = C(w_in[:, qcols]); d["w_ng"] = C(w_in[:, o_ng:o_ng + 24])
    d["b_u"] = C(b_in[None, o_u:o_u + 512]); d["b_v"] = C(b_in[None, o_v:o_v + 512]); d["b_q"] = C(b_in[qcols][None]); d["b_ng"] = C(b_in[None, o_ng:o_ng + 24])
    d["b_qc"] = C(b_in[o_q:o_q + 512].reshape(2, 4, 64).transpose(0, 2, 1).reshape(128, 4))
    d["w_mg"] = C(w_in[:, o_mg:o_mg + 2048]); d["b_mg"] = C(b_in[None, o_mg:o_mg + 2048]); d["b_mgc"] = C(b_in[o_mg:o_mg + 2048].reshape(16, 128).T)
    d["lng"] = rep128(P["gm_ln_g"][l]); d["lnb"] = rep128(P["gm_ln_b"][l])
    d["wsT"] = C(P["gm_ws"][l].transpose(2, 0, 1))
    d["triu"] = sh["triu"]
    bs = P["gm_bs"][l]
    d["bsb"] = C(np.stack([np.concatenate([np.broadcast_to(bs[2 * m][None], (64, 128)), np.broadcast_to(bs[2 * m + 1][None], (64, 128))], 0) for m in range(4)], 1))
    w1 = P["cmp_w1"][l].reshape(2, 32, 64, 256).transpose(2, 0, 1, 3)
    d["w1d"] = C(np.concatenate([w1, w1], 0))
    pT = np.zeros((64, 2, 34), f32)
    pT[:, :, :32] = P["cmp_pos"][l].transpose(2, 0, 1)
    d["posT"] = pT
    d["b1"] = C(P["cmp_b1"][l].reshape(2, 2, 128).transpose(2, 0, 1))
    w2k = P["cmp_w2"][l][0].reshape(2, 128, 64).transpose(1, 0, 2)
    d["w2k"] = C(np.concatenate([w2k, w2k], 2))
    d["w2v"] = C(P["cmp_w2"][l][1].reshape(2, 128, 64).transpose(1, 0, 2))
    d["ovl"] = sh["ovl"]
    d["w_pa"] = C(P["w_proj_a"][l]); d["w_pb"] = C(P["w_proj_b"][l]); d["w_o"] = C(P["w_out"][l])
    d["ln1g"] = rep128(P["ln1_g"][l]); d["ln1b"] = rep128(P["ln1_b"][l])
    d["jw"] = sh["jw"]; d["i4"] = sh["i4"]
    rb = P["rel_bias"]
    d["far"] = C(np.stack([np.broadcast_to(np.repeat(rb[31, 4 * g:4 * g + 4], 128)[None], (128, 512)) for g in range(2)]))
    return d


def core_inputs_B(j, P):
    cs = core_consts(j)
    rb = P["rel_bias"]
    C = np.ascontiguousarray

    def gath(idx, g):
        return np.concatenate([rb[idx, 4 * g + hp] for hp in range(4)], axis=1)
    d = {}
    d["bsel"] = C(np.stack([np.stack([gath(ix, g) for ix in cs["isel"]]) for g in range(2)]))
    d["bwin"] = C(np.stack([np.stack([gath(ix, g) for ix in cs["iwin"]]) for g in range(2)]))
    d["bcmp"] = C(np.stack([gath(cs["icmp"], g) for g in range(2)]))
    d["nsel"] = cs["nsel"]; d["nwin"] = cs["nwin"]; d["ncmp"] = cs["ncmp"]
    d["cand"] = cs["cand"]; d["forc"] = cs["forc"]
    return d


_PROG = {}


def get_prog(name):
    if name not in _PROG:
        _PROG[name] = build_B() if name == "B" else build_C()
    return _PROG[name]


def run_B(l, h, P):
    nc = get_prog("B")
    wl = layer_inputs_B(l, P)
    in_maps = []
    for c in range(8):
        b, j = c // 4, c % 4
        idx = own_index(j)
        d = dict(wl)
        d.update(core_inputs_B(j, P))
        d["hT_full"] = np.ascontiguousarray(h[b].T)
        d["hT_own"] = np.ascontiguousarray(h[b][idx].T)
        d["h_own"] = np.ascontiguousarray(h[b][idx])
        in_maps.append(d)
    res = run_bass_kernel_spmd(nc, in_maps, core_ids=list(range(8)))
    out = np.empty_like(h)
    for c in range(8):
        b, j = c // 4, c % 4
        out[b][own_index(j)] = res.results[c]["h1_out"]
    return out


def run_C(l, h1, P):
    nc = get_prog("C")
    C = np.ascontiguousarray
    cwt = np.stack([P["ffn_conv_w"][l][0], P["ffn_conv_w"][l][1], P["ffn_conv_w"][l][2], P["ffn_conv_b"][l]], 1)
    wl = {"w_up": C(P["ffn_w_up"][l]), "w_dn": C(P["ffn_w_down"][l]), "cw": C(cwt.reshape(44, 128, 4).transpose(1, 0, 2)),
          "ln2g": rep128(P["ln2_g"][l]), "ln2b": rep128(P["ln2_b"][l])}
    in_maps = []
    for c in range(8):
        b, j = c // 4, c % 4
        idx = own_index(j)
        hp = np.concatenate([np.zeros((2, D), np.float32), h1[b]], 0)
        blocks = np.stack([hp[128 * (4 * s + j):128 * (4 * s + j) + 130] for s in range(NB)])
        d = dict(wl)
        d["h1T"] = C(blocks.transpose(2, 0, 1))
        d["h1"] = C(h1[b][idx])
        in_maps.append(d)
    res = run_bass_kernel_spmd(nc, in_maps, core_ids=list(range(8)))
    out = np.empty_like(h1)
    for c in range(8):
        b, j = c // 4, c % 4
        out[b][own_index(j)] = res.results[c]["h2_out"]
    return out


def kernel(**inputs):
    P = {k: np.asarray(v, dtype=np.float32) for k, v in inputs.items()}
    h = P["x"]
    for l in range(2):
        h1 = run_B(l, h, P)
        h = run_C(l, h1, P)
    return h.astype(np.float32)
```
